# Optimizing a Trainium2 kernel written in Bass

```python
import jax, jax.numpy as jnp
from jax import lax
import numpy as np

D_MODEL = 4096
BATCH = 2
SEQ = 8192
DEPTH = 2

D_MIX = D_MODEL
HEAD_DIM = 128
W_A = 3 * D_MIX // 8
W_B = 5 * D_MIX // 16
W_C = D_MIX - W_A - W_B
H_A = W_A // HEAD_DIM
H_B = W_B // HEAD_DIM
CONV_WIDTH = 31
CHUNK = 128
POOL_WINDOWS = (2, 4, 8, 16)
N_POOL_GROUPS = len(POOL_WINDOWS)
C_GROUP = W_C // N_POOL_GROUPS
EPS = 1e-6
COL_WIDTHS = (W_A, W_A, W_A, W_B, W_B, W_B, W_C, W_C)
P_IN = 3 * W_A + 3 * W_B + 2 * W_C
SPLITS = (W_A, 2 * W_A, 3 * W_A, 3 * W_A + W_B, 3 * W_A + 2 * W_B, 3 * W_A + 3 * W_B, 3 * W_A + 3 * W_B + W_C)

kernel_name = "hybrid_conv_gmlp_pool_parallel_heads"


def _rmsnorm(x, g):
    xf = x.astype(jnp.float32)
    r = lax.rsqrt(jnp.mean(xf * xf, axis=-1, keepdims=True) + EPS)
    return (xf * r * g.astype(jnp.float32)).astype(x.dtype)


def _layernorm(x, g, b):
    xf = x.astype(jnp.float32)
    mu = jnp.mean(xf, axis=-1, keepdims=True)
    var = jnp.mean(jnp.square(xf - mu), axis=-1, keepdims=True)
    y = (xf - mu) * lax.rsqrt(var + EPS) * g.astype(jnp.float32) + b.astype(jnp.float32)
    return y.astype(x.dtype)


def _conformer_conv_branch(a_val, a_glu, conv_w, conv_b, ln_g, ln_b, pw):
    h = a_val * jax.nn.sigmoid(a_glu)
    h = lax.conv_general_dilated(
        h, conv_w[:, None, :].astype(h.dtype), window_strides=(1,),
        padding=[(CONV_WIDTH - 1, 0)],
        dimension_numbers=("NWC", "WIO", "NWC"),
        feature_group_count=h.shape[-1]) + conv_b
    h = jax.nn.silu(_layernorm(h, ln_g, ln_b))
    return h @ pw


def _gmlp_chunk_branch(u, v, ln_g, ln_b, ws, bias):
    b, s, c = v.shape
    u = jax.nn.gelu(u)
    v = _layernorm(jax.nn.gelu(v), ln_g, ln_b)
    vc = v.reshape(b, s // CHUNK, CHUNK, H_B, HEAD_DIM)
    ws_causal = jnp.tril(ws)
    sp = jnp.einsum('hpq,bnqhd->bnphd', ws_causal, vc) + bias.T[None, None, :, :, None]
    return u * sp.reshape(b, s, c)


def _pool_branch(c_in, c_w, c_scale):
    b, s, c = c_in.shape
    cf = c_in.astype(jnp.float32)
    cs = jnp.cumsum(cf, axis=1)
    count = jnp.arange(1, s + 1, dtype=jnp.float32)[:, None]
    diffs = []
    for gi, w in enumerate(POOL_WINDOWS):
        sl = slice(gi * C_GROUP, (gi + 1) * C_GROUP)
        csg = cs[..., sl]
        lag = jnp.pad(csg, ((0, 0), (w, 0), (0, 0)))[:, :s]
        mean = (csg - lag) / jnp.minimum(count, float(w))
        diffs.append(mean - cf[..., sl])
    d = jnp.stack(diffs, axis=2).astype(c_in.dtype)
    y = jnp.einsum('bsgc,gcd->bsgd', d, c_w).reshape(b, s, c)
    return y * c_scale


def setup_inputs(seed: int = 0) -> dict:
    key = jax.random.key(seed)
    ks = jax.random.split(key, 20)
    f32 = jnp.float32
    nrm = lambda k, shape, scale: jax.random.normal(k, shape, f32) * scale
    return {
        "x": nrm(ks[0], (BATCH, SEQ, D_MODEL), 1.0),
        "norm_g": 1.0 + nrm(ks[1], (DEPTH, D_MODEL), 0.02),
        "w_in": nrm(ks[2], (DEPTH, D_MODEL, P_IN), D_MODEL ** -0.5),
        "conv_w": nrm(ks[3], (DEPTH, CONV_WIDTH, W_A), CONV_WIDTH ** -0.5),
        "conv_b": nrm(ks[4], (DEPTH, W_A), 0.02),
        "a_ln_g": 1.0 + nrm(ks[5], (DEPTH, W_A), 0.02),
        "a_ln_b": nrm(ks[6], (DEPTH, W_A), 0.02),
        "a_pw": nrm(ks[7], (DEPTH, W_A, W_A), W_A ** -0.5),
        "b_ln_g": 1.0 + nrm(ks[8], (DEPTH, W_B), 0.02),
        "b_ln_b": nrm(ks[9], (DEPTH, W_B), 0.02),
        "b_ws": nrm(ks[10], (DEPTH, H_B, CHUNK, CHUNK), CHUNK ** -0.5),
        "b_bias": 1.0 + nrm(ks[11], (DEPTH, H_B, CHUNK), 0.02),
        "c_w": nrm(ks[12], (DEPTH, N_POOL_GROUPS, C_GROUP, C_GROUP), C_GROUP ** -0.5),
        "c_scale": 1.0 + nrm(ks[13], (DEPTH, W_C), 0.1),
        "w_out": nrm(ks[14], (DEPTH, D_MIX, D_MODEL), D_MIX ** -0.5),
        "final_g": 1.0 + nrm(ks[15], (D_MODEL,), 0.02),
    }


def reference(x, norm_g, w_in, conv_w, conv_b, a_ln_g, a_ln_b, a_pw, b_ln_g, b_ln_b,
              b_ws, b_bias, c_w, c_scale, w_out, final_g):
    for l in range(DEPTH):
        h = _rmsnorm(x, norm_g[l])
        proj = h @ w_in[l]
        a_val, a_glu, a_gate, b_u, b_v, b_gate, c_in, c_gate = jnp.split(proj, SPLITS, axis=-1)
        ya = _conformer_conv_branch(a_val, a_glu, conv_w[l], conv_b[l], a_ln_g[l], a_ln_b[l], a_pw[l])
        ya = ya * jax.nn.silu(a_gate)
        yb = _gmlp_chunk_branch(b_u, b_v, b_ln_g[l], b_ln_b[l], b_ws[l], b_bias[l])
        yb = yb * jax.nn.silu(b_gate)
        yc = _pool_branch(c_in, c_w[l], c_scale[l])
        yc = yc * jax.nn.silu(c_gate)
        y = jnp.concatenate([ya, yb, yc], axis=-1) @ w_out[l]
        x = x + y
    return _rmsnorm(x, final_g)
```

```python
import contextlib
import numpy as np
import concourse.bass as bass
import concourse.mybir as mybir
from concourse.bass_utils import run_bass_kernel_spmd

F32 = mybir.dt.float32
F32R = mybir.dt.float32r
BF16 = mybir.dt.bfloat16
AF = mybir.ActivationFunctionType
ALU = mybir.AluOpType

D = 4096
L = 2
WA, WB, WC = 1536, 1280, 1280
NCORE = 8
TOK = 2048
HALO = 128
LTOK = TOK + HALO
TMAX = 384
TILES = [(0, 384), (384, 384), (768, 384), (1152, 384), (1536, 384), (1920, 256)]
NW = 5
EPS = 1e-6
CONVW = 31
HH = 32
CH = 16
POOLW = (2, 4, 8, 16)

S_AVAL, S_AGLU, S_AGATE, S_BU, S_BV, S_BGATE, S_CIN, S_CGATE = 0, 12, 24, 36, 46, 56, 66, 76

def _chunk_groups(j):
    lo, hi = j * 128, j * 128 + 127
    return sorted(set([lo // 320, hi // 320]))
CW_JI = []
for _jo in range(10):
    gs = _chunk_groups(_jo)
    CW_JI.append([ji for ji in range(10) if set(_chunk_groups(ji)) & set(gs)])

def _pp_layout():
    off = {}
    o = 0
    for name, n in (("cw", L * 12 * CONVW), ("cb", L * 12), ("alg", L * 12), ("alb", L * 12),
                    ("blg", L * 10), ("blb", L * 10), ("csc", L * 10), ("ng", L * 32), ("fg", 32), ("invw", 10)):
        off[name] = o
        o += n
    return off, o
PP_OFF, NPP = _pp_layout()
PP_OFF["cwh"] = PP_OFF["cw"]
NX = 0
MISC_N = 128 + 128 + 160


class Stream:
    def __init__(self, name, sem, serial=False):
        self.name = name
        self.sem = sem
        self.serial = serial
        self.count = 0
        self.ops = []
        self.seen = {}

    def emit(self, _opname, deps=(), inc=True, dma=None, **kw):
        fn = (_opname, kw)
        waits = []
        deps = list(deps)
        if self.serial and self.count > 0 and dma is None:
            deps.append((self.name, self.count, self.sem))
        for d in deps:
            if d is None:
                continue
            key, val, sem = d
            if self.seen.get(key, 0) >= val:
                continue
            self.seen[key] = val
            waits.append((sem, val))
        tk = None
        if dma is not None:
            tk = dma
        elif inc:
            self.count += 1
            tk = (self.name, self.count, self.sem)
        self.ops.append((waits, fn, inc, dma))
        return tk

    def replay(self, eng):
        for waits, fn, inc, dma in self.ops:
            for sem, val in waits:
                eng.wait_ge(sem, val)
            ins = getattr(eng, fn[0])(**fn[1])
            if dma is not None:
                ins.then_inc(dma[2], 16)
            elif inc:
                ins.then_inc(self.sem, 1)


class Ring:
    def __init__(self, bufs):
        self.bufs = bufs
        self.i = 0
        self.rel = [[] for _ in bufs]

    def get(self):
        k = self.i % len(self.bufs)
        self.i += 1
        d = self.rel[k]
        self.rel[k] = []
        return k, self.bufs[k], d

    def release(self, k, *tickets):
        self.rel[k].extend(t for t in tickets if t is not None)


def build_nc(tiles=None, nlay=L, ltok=LTOK):
    tiles = TILES if tiles is None else tiles
    nc = bass.Bass("TRN2", target_bir_lowering=False)
    xT = nc.dram_tensor("xT", [32, 128, ltok], F32, kind="ExternalInput").ap()
    win = nc.dram_tensor("win", [L, 86, 128, 4096], F32, kind="ExternalInput").ap()
    wpw = nc.dram_tensor("wpw", [L, 12, 128, 1536], F32, kind="ExternalInput").ap()
    wcw = nc.dram_tensor("wcw", [L, 10, 128, 640], F32, kind="ExternalInput").ap()
    wout = nc.dram_tensor("wout", [L, 32, 128, 4096], F32, kind="ExternalInput").ap()
    wsr = nc.dram_tensor("wsr", [L, 128, 1280], F32, kind="ExternalInput").ap()
    ppd = nc.dram_tensor("pp", [128, NPP], F32, kind="ExternalInput").ap()
    biasd = nc.dram_tensor("biasb", [L, 128, 1280], F32, kind="ExternalInput").ap()
    miscd = nc.dram_tensor("misc", [128, MISC_N], F32, kind="ExternalInput").ap()
    outT = nc.dram_tensor("outT", [32, 128, ltok - HALO], F32, kind="ExternalOutput").ap()

    es = contextlib.ExitStack()
    with es:
        def sb(name, shape, dt):
            return es.enter_context(nc.sbuf_tensor(name, shape, dt))

        def ps(name, shape, dt=F32):
            return es.enter_context(nc.psum_tensor(name, shape, dt))

        def sem(name):
            return es.enter_context(nc.semaphore(name))

        x = sb("x", [128, 32, TMAX], F32)
        hT = sb("hT", [128, 32, TMAX], BF16)
        ycat = sb("ycat", [128, 32, TMAX], BF16)
        wb = [sb("w%d" % i, [128, 4096], BF16) for i in range(NW)]
        co = sb("co", [128, 12, TMAX], F32)
        hn = sb("hn", [128, 12, TMAX], BF16)
        hbuf = sb("hbuf", [128, HH + TMAX], F32)
        cbufs = [sb("cbuf%d" % i, [128, CH + TMAX], F32) for i in range(2)]
        ptA = sb("ptA", [128, CH + TMAX], F32)
        ptB = sb("ptB", [128, CH + TMAX], F32)
        stg = [sb("stg%d" % i, [128, TMAX], F32) for i in range(6)]
        rstd = sb("rstd", [128, TMAX], F32)
        nb = sb("nb", [128, TMAX], F32)
        tmpv = sb("tmpv", [128, TMAX], F32)
        nacc = sb("nacc", [128, TMAX], F32)
        mean = nacc
        nrstd = sb("nrstd", [128, TMAX], F32)
        vnb = [sb("vnb%d" % i, [128, TMAX], BF16) for i in range(2)]
        vnT = [sb("vnT%d" % i, [128, TMAX], BF16) for i in range(2)]
        htail = sb("htail", [128, L, 12, HH], F32)
        ctail = sb("ctail", [128, L, 10, CH], F32)
        wsT = sb("wsT", [128, L, 1280], BF16)
        bbuf = [sb("bb%d" % i, [128, 128], F32) for i in range(2)]
        pp = sb("pp_sb", [128, NPP + NX], F32)
        corr_sb = sb("corr_sb", [128, 160], F32)
        identb = sb("identb", [128, 128], BF16)
        onesA = sb("onesA", [128, 128], F32)
        onesB = sb("onesB", [128, 128], F32)
        onesN = sb("onesN", [128, 128], F32)

        print('sbuf bytes remaining', nc.sbuf_bytes_remaining)
        dC = ycat

        proj = [ps("pj%d" % i, [128, 512]) for i in range(3)]
        aux = [ps("ax%d" % i, [128, 512]) for i in range(2)]
        st0 = ps("st0", [128, 512])
        st1 = ps("st1", [128, 512])
        tps = ps("tps", [128, 1024], BF16)

        s_pe, s_act, s_dve, s_pool = sem("s_pe"), sem("s_act"), sem("s_dve"), sem("s_pool")
        s_w = [sem("s_w%d" % i) for i in range(NW)]
        s_p = sem("s_p")
        s_bb = [sem("s_bb%d" % i) for i in range(2)]
        bb_cnt = [0, 0]
        s_pre = [sem("s_pre%d" % i) for i in range(6)]
        pre_cnt = [0] * 6
        s_xg = [sem("s_x%d" % i) for i in range(4)]
        s_og = [sem("s_o%d" % i) for i in range(4)]

        PE = Stream("pe", s_pe)
        ACT = Stream("act", s_act, serial=True)
        DVE = Stream("dve", s_dve, serial=True)
        POOL = Stream("pool", s_pool, serial=True)
        SP = Stream("sp", None)

        PROJ = Ring(proj)
        AUX = Ring(aux)
        STG = Ring(stg)

        def ppc(name, idx):
            o = PP_OFF[name] + idx
            return pp[:, o:o + 1]

        plan = []

        def plan_cout(l, jo):
            plan.append((win[l, S_CGATE + jo], 4096))
            plan.append((wcw[l, jo, :, 0:len(CW_JI[jo]) * 128], len(CW_JI[jo]) * 128))

        for ti in range(len(tiles)):
            for l in range(nlay):
                for j in range(12):
                    if j >= 5:
                        plan_cout(l, j - 5)
                    plan.append((win[l, S_AGLU + j], 4096))
                    plan.append((win[l, S_AVAL + j], 4096))
                    if j < 10:
                        plan.append((win[l, S_CIN + j], 4096))
                for jo in (7, 8, 9):
                    plan_cout(l, jo)
                for j in range(10):
                    plan.append((win[l, S_BV + j], 4096))
                for j in range(12):
                    plan.append((win[l, S_AGATE + j], 4096))
                    plan.append((wpw[l, j], 1536))
                for j in range(10):
                    plan.append((win[l, S_BU + j], 4096))
                    plan.append((win[l, S_BGATE + j], 4096))
                for n in range(32):
                    plan.append((wout[l, n], 4096))

        class WMgr:
            def __init__(self):
                self.issued = 0
                self.consumed = 0
                self.free_tk = [None] * NW
                self.fill = [0] * NW
                self.load_tk = {}

            def issue_upto(self, n):
                while self.issued < min(n, len(plan)):
                    k = self.issued
                    b = k % NW
                    src, ncols = plan[k]
                    self.fill[b] += 1
                    tk = ("w%d" % b, 16 * self.fill[b], s_w[b])
                    dst = wb[b][:, 0:ncols]
                    POOL.emit("dma_start", **dict(out=dst, in_=src, max_dma_last_dim=8192),
                              deps=[self.free_tk[b]], dma=tk)
                    self.load_tk[k] = tk
                    self.issued += 1

            def take(self, ncols):
                k = self.consumed
                self.issue_upto(k + 1)
                assert plan[k][1] == ncols, (k, plan[k][1], ncols)
                return k, wb[k % NW], self.load_tk.pop(k)

            def done(self, k, tk):
                self.free_tk[k % NW] = tk
                self.consumed += 1
                self.issue_upto(k + NW + 1)

        W = WMgr()

        deferred = []
        slab_ctr = [0]

        def defer(delay, fn):
            deferred.append((slab_ctr[0] + delay, fn))

        def flush(force=False):
            while deferred and (force or deferred[0][0] <= slab_ctr[0]):
                _, fn = deferred.pop(0)
                fn()

        def slab_mm(out_ap, ncols, rhs_list, deps, deps_i=None):
            k, buf, ltk = W.take(ncols)
            n = len(rhs_list)
            assert n * 128 == ncols
            tk = None
            for i, rhs in enumerate(rhs_list):
                tk = PE.emit("matmul", **dict(out=out_ap, lhsT=buf[:, i * 128:(i + 1) * 128], rhs=rhs,
                                                                     start=(i == 0), stop=(i == n - 1)),
                             deps=(([ltk] + list(deps)) if i == 0 else []) + ([deps_i[i]] if deps_i else []), inc=(i == n - 1))
            W.done(k, tk)
            slab_ctr[0] += 1
            return tk

        tkp = ("s_p", 16 * 5, s_p)
        SP.emit("dma_start", **dict(out=pp[:, 0:NPP], in_=ppd), dma=("s_p", 16, s_p))
        SP.emit("dma_start", **dict(out=stg[0][:, 0:256], in_=miscd[:, 0:256]), dma=("s_p", 32, s_p))
        SP.emit("dma_start", **dict(out=corr_sb[:], in_=miscd[:, 256:416]), dma=("s_p", 48, s_p))
        costage = co[:, 0:8, 0:320]
        SP.emit("dma_start", **dict(out=co[:, 0:4, 0:320], in_=wsr[0].rearrange("q (a b) -> q a b", a=4)), dma=("s_p", 64, s_p))
        SP.emit("dma_start", **dict(out=co[:, 4:8, 0:320], in_=wsr[1].rearrange("q (a b) -> q a b", a=4)), dma=tkp)
        W.issue_upto(NW)
        tk0 = DVE.emit("tensor_copy", **dict(out=identb[:], in_=stg[0][:, 0:128]), deps=[tkp])
        DVE.emit("memset", **dict(ap=onesA[:], constant=1.0 / WA))
        DVE.emit("memset", **dict(ap=onesB[:], constant=1.0 / WB))
        DVE.emit("memset", **dict(ap=onesN[:], constant=1.0 / D))
        DVE.emit("memset", **dict(ap=htail[:], constant=0.0))
        tk_init = DVE.emit("memset", **dict(ap=ctail[:], constant=0.0))
        tk_half = DVE.emit("tensor_scalar", out=pp[:, PP_OFF["cw"]:PP_OFF["cw"] + L * 12 * CONVW],
                           in0=pp[:, PP_OFF["cw"]:PP_OFF["cw"] + L * 12 * CONVW], scalar1=0.5, scalar2=None, op0=ALU.mult, deps=[tkp])
        tk_ws = None
        for l in range(L):
            for h in range(10):
                a, r = divmod(h * 128, 320)
                for p0 in range(0, 128, 64):
                    col = h * 128 + p0
                    a, r = divmod(col, 320)
                    tk_ws = DVE.emit("tensor_tensor", **dict(
                        out=wsT[:, l, col:col + 64], in0=co[:, 4 * l + a, r:r + 64], in1=stg[0][:, 128 + p0:128 + p0 + 64], op=ALU.mult),
                        deps=[tkp])
        STG.release(0, tk_ws)
        co_free = [tk_ws]
        hT_free = [None]
        ycat_free = [None]
        stat_free = [[], []]
        htail_tk = [[tk_init] * 12 for _ in range(L)]
        ctail_tk = [[tk_init] * 10 for _ in range(L)]
        x_state = {"loads": 0, "stores": 0, "store_tk": [None] * 4}

        def norm_accum(n, T, xtk, st):
            k, sbuf_, sdeps = STG.get()
            tsq = ACT.emit("activation", **dict(out=sbuf_[:, :T], in_=x[:, n, :T], func=AF.Square), deps=[xtk] + sdeps)
            if n == 0:
                t = DVE.emit("tensor_copy", **dict(out=nacc[:, :T], in_=sbuf_[:, :T]), deps=[tsq] + st["nacc_free"])
            else:
                t = DVE.emit("tensor_tensor", **dict(out=nacc[:, :T], in0=nacc[:, :T], in1=sbuf_[:, :T], op=ALU.add),
                             deps=[tsq, st["nacc_tk"]])
            st["nacc_tk"] = t
            STG.release(k, t)

        def norm_finalize(T, st, acc=None, acc_tk=None):
            if acc is None:
                acc, acc_tk = nacc, st["nacc_tk"]
            tpe = PE.emit("matmul", **dict(out=st0[:, :T], lhsT=onesN[:], rhs=acc[:, :T], start=True, stop=True),
                          deps=[acc_tk] + stat_free[0])
            stat_free[0] = []
            if acc is nacc:
                st["nacc_free"] = [tpe]
            ta = ACT.emit("activation", **dict(out=tmpv[:, :T], in_=st0[:, :T], func=AF.Sqrt, bias=EPS, scale=1.0),
                          deps=[tpe] + st.get("tmpv_free", []))
            stat_free[0].append(ta)
            td = DVE.emit("reciprocal", **dict(out=nrstd[:, :T], in_=tmpv[:, :T]), deps=[ta])
            st["tmpv_free"] = [td]
            return td

        nst = {"nacc_free": [], "nacc_tk": None}
        pers = {}

        for ti, (t0, T) in enumerate(tiles):
            C = T // 128
            x_state["loads"] += 1
            xg_tk = []
            for q in range(4):
                tkq = ("s_x%d" % q, 16 * x_state["loads"], s_xg[q])
                SP.emit("dma_start", **dict(out=x[:, 8 * q:8 * q + 8, 0:T],
                                            in_=xT[8 * q:8 * q + 8, :, t0:t0 + T].rearrange("c p t -> p c t")),
                        deps=[x_state["store_tk"][q]], dma=tkq)
                xg_tk.append(tkq)
            if pers.get("pre_tk") is not None:
                rs_tk = norm_finalize(T, nst, acc=rstd, acc_tk=pers["pre_tk"])
                pers["pre_tk"] = None
            else:
                for n in range(32):
                    norm_accum(n, T, xg_tk[n // 8], nst)
                rs_tk = norm_finalize(T, nst)
            x_tk = [xg_tk[n // 8] for n in range(32)]

            for l in range(nlay):
                btks = []
                for j in range(10):
                    bb_cnt[j % 2] += 1
                    btk_j = ("s_bb%d" % (j % 2), 16 * bb_cnt[j % 2], s_bb[j % 2])
                    btks.append(btk_j)

                ht_tk = None
                ht_tks = []
                for n in range(32):
                    ht_tk = DVE.emit("scalar_tensor_tensor", **dict(
                        out=hT[:, n, :T], in0=x[:, n, :T], scalar=ppc("ng", l * 32 + n), in1=nrstd[:, :T],
                        op0=ALU.mult, op1=ALU.mult), deps=[rs_tk, x_tk[n], hT_free[0], tkp])
                    ht_tks.append(ht_tk)
                hT_rhs = [hT[:, kc, :T] for kc in range(32)]

                conv_tk = [None] * 12
                dC_tk = [None] * 10
                lnA = {}

                def ln_stats(j, nch, ones, src_tk, holder, T=T):
                    k, sbuf_, sdeps = STG.get()
                    tsq = ACT.emit("activation", out=sbuf_[:, :T], in_=co[:, j, :T], func=AF.Square, deps=[src_tk] + sdeps)
                    if j == 0:
                        a1 = POOL.emit("tensor_copy", out=nacc[:, :T], in_=co[:, j, :T],
                                       deps=[src_tk, pers.get("mean_free")] + nst["nacc_free"])
                        a2 = POOL.emit("tensor_copy", out=nb[:, :T], in_=sbuf_[:, :T], deps=[tsq, pers.get("nb_free")])
                    else:
                        a1 = POOL.emit("tensor_tensor", out=nacc[:, :T], in0=nacc[:, :T], in1=co[:, j, :T], op=ALU.add, deps=[src_tk])
                        a2 = POOL.emit("tensor_tensor", out=nb[:, :T], in0=nb[:, :T], in1=sbuf_[:, :T], op=ALU.add, deps=[tsq])
                    STG.release(k, a2)
                    if j == nch - 1:
                        d0, d1 = stat_free[0], stat_free[1]
                        stat_free[0] = []
                        stat_free[1] = []
                        PE.emit("matmul", out=st0[:, :T], lhsT=ones[:], rhs=nacc[:, :T], start=True, stop=True, deps=[a1] + d0, inc=False)
                        te = PE.emit("matmul", out=st1[:, :T], lhsT=ones[:], rhs=nb[:, :T], start=True, stop=True, deps=[a2] + d1)
                        holder["pe"] = te
                        nst["nacc_free"] = [te]

                def lnA_stats(j):
                    ln_stats(j, 12, onesA, conv_tk[j], lnA)

                def ln_finalize(pe_tk, T):
                    t1 = DVE.emit("tensor_copy", **dict(out=mean[:, :T], in_=st0[:, :T]), deps=[pe_tk] + nst["nacc_free"])
                    t2 = DVE.emit("tensor_tensor", **dict(out=tmpv[:, :T], in0=mean[:, :T], in1=mean[:, :T], op=ALU.mult),
                                  deps=[t1] + nst.get("tmpv_free", []))
                    t3 = DVE.emit("tensor_tensor", **dict(out=tmpv[:, :T], in0=st1[:, :T], in1=tmpv[:, :T], op=ALU.subtract),
                                  deps=[t2])
                    t3b = DVE.emit("tensor_scalar", **dict(out=tmpv[:, :T], in0=tmpv[:, :T], scalar1=0.0, scalar2=None, op0=ALU.max),
                                   deps=[t3])
                    ta = ACT.emit("activation", **dict(out=rstd[:, :T], in_=tmpv[:, :T], func=AF.Sqrt, bias=EPS, scale=1.0),
                                  deps=[t3b])
                    t4 = DVE.emit("reciprocal", **dict(out=rstd[:, :T], in_=rstd[:, :T]), deps=[ta])
                    t5 = DVE.emit("scalar_tensor_tensor", **dict(out=nb[:, :T], in0=mean[:, :T], scalar=-1.0, in1=rstd[:, :T],
                                                                   op0=ALU.mult, op1=ALU.mult), deps=[t4])
                    stat_free[0].append(t1)
                    stat_free[1].append(t3)
                    nst["tmpv_free"] = [ta]
                    pers["mean_free"] = t5
                    return t5

                def lnA_finish(T=T, l=l):
                    t5 = ln_finalize(lnA["pe"], T)
                    last = None
                    for j in range(12):
                        a = DVE.emit("tensor_tensor", **dict(out=co[:, j, :T], in0=co[:, j, :T], in1=rstd[:, :T], op=ALU.mult),
                                     deps=[t5, lnA["pe"]])
                        b = DVE.emit("tensor_tensor", **dict(out=co[:, j, :T], in0=co[:, j, :T], in1=nb[:, :T], op=ALU.add),
                                     deps=[a])
                        pers["nb_free"] = b
                        last = lnA.setdefault("hn_tks", [None] * 12)[j] = ACT.emit("activation", **dict(out=hn[:, j, :T], in_=co[:, j, :T], func=AF.Silu,
                                                                    bias=ppc("alb", l * 12 + j), scale=ppc("alg", l * 12 + j)),
                                        deps=[b, pers.get("hn_free")])
                    lnA["hn_tk"] = last

                cw_state = {"last": None}
                yc_tk = [None] * 32

                def c_out(jo, T=T, l=l):
                    pk, pbuf, pdeps = PROJ.get()
                    tpe = slab_mm(pbuf[:, :T], 4096, hT_rhs, pdeps)
                    sk, sbuf_, sdeps = STG.get()
                    tsg = ACT.emit("activation", out=sbuf_[:, :T], in_=pbuf[:, :T], func=AF.Silu, deps=[tpe] + sdeps)
                    PROJ.release(pk, tsg)
                    ak, abuf, adeps = AUX.get()
                    jis = CW_JI[jo]
                    tcw = slab_mm(abuf[:, :T], len(jis) * 128, [dC[:, 12 + ji, :T] for ji in jis],
                                  [dC_tk[ji] for ji in jis] + adeps)
                    cw_state["last"] = tcw
                    ty = DVE.emit("scalar_tensor_tensor", out=ycat[:, 22 + jo, :T], in0=abuf[:, :T], scalar=ppc("csc", l * 10 + jo),
                                  in1=sbuf_[:, :T], op0=ALU.mult, op1=ALU.mult, deps=[tcw, tsg, ycat_free[0], tkp])
                    yc_tk[22 + jo] = ty
                    AUX.release(ak, ty)
                    STG.release(sk, ty)
                    flush()

                for j in range(12):
                    if j >= 5:
                        c_out(j - 5)
                    pk, pbuf, pdeps = PROJ.get()
                    tpe = slab_mm(pbuf[:, :T], 4096, hT_rhs, pdeps, deps_i=(ht_tks if j == 0 else None))
                    sk, sbuf_, sdeps = STG.get()
                    tsig = ACT.emit("activation", out=sbuf_[:, :T], in_=pbuf[:, :T], func=AF.Tanh, scale=0.5, deps=[tpe] + sdeps)
                    PROJ.release(pk, tsig)
                    flush()
                    pk2, pbuf2, pdeps2 = PROJ.get()
                    tpe2 = slab_mm(pbuf2[:, :T], 4096, hT_rhs, pdeps2)
                    skv, vbuf_, sdepsv = STG.get()
                    tval = ACT.emit("activation", out=vbuf_[:, :T], in_=pbuf2[:, :T], func=AF.Copy, deps=[tpe2] + sdepsv)
                    PROJ.release(pk2, tval)
                    th0 = DVE.emit("tensor_copy", out=hbuf[:, 0:HH], in_=htail[:, l, j, :], deps=[htail_tk[l][j]])
                    tglu = DVE.emit("scalar_tensor_tensor", out=hbuf[:, HH:HH + T], in0=sbuf_[:, :T], scalar=1.0, in1=vbuf_[:, :T],
                                    op0=ALU.add, op1=ALU.mult, deps=[tval, tsig])
                    STG.release(sk, tglu)
                    STG.release(skv, tglu)
                    htail_tk[l][j] = DVE.emit("tensor_copy", out=htail[:, l, j, :], in_=hbuf[:, T:T + HH], deps=[tglu])
                    base = (l * 12 + j) * CONVW
                    tk = DVE.emit("tensor_scalar", out=co[:, j, :T], in0=hbuf[:, 2:2 + T], scalar1=ppc("cwh", base),
                                  scalar2=ppc("cb", l * 12 + j), op0=ALU.mult, op1=ALU.add, deps=[th0, tglu, tk_half] + co_free)
                    co_free = []
                    for k in range(1, CONVW):
                        tk = DVE.emit("scalar_tensor_tensor", out=co[:, j, :T], in0=hbuf[:, 2 + k:2 + k + T], scalar=ppc("cwh", base + k),
                                      in1=co[:, j, :T], op0=ALU.mult, op1=ALU.add, deps=[tk])
                    conv_tk[j] = tk
                    defer(9, lambda j=j: lnA_stats(j))
                    flush()
                    if j < 10:
                        pk3, pbuf3, pdeps3 = PROJ.get()
                        tpe3 = slab_mm(pbuf3[:, :T], 4096, hT_rhs, pdeps3)
                        cbuf = cbufs[j % 2]
                        tc1 = ACT.emit("activation", out=cbuf[:, CH:CH + T], in_=pbuf3[:, :T], func=AF.Copy,
                                       deps=[tpe3, pers.get("cbuf_free%d" % (j % 2))])
                        PROJ.release(pk3, tc1)
                        tc0 = POOL.emit("tensor_copy", out=cbuf[:, 0:CH], in_=ctail[:, l, j, :],
                                        deps=[ctail_tk[l][j], pers.get("cbuf_free%d" % (j % 2))])
                        ctail_tk[l][j] = POOL.emit("tensor_copy", out=ctail[:, l, j, :], in_=cbuf[:, T:T + CH], deps=[tc1])
                        gs = _chunk_groups(j)
                        parts = [(0, 128, gs[0])] if len(gs) == 1 else [(0, 64, gs[0]), (64, 128, gs[1])]
                        E = CH + T
                        srcs = [cbuf, ptA, ptB, ptA, ptB]
                        prev = [tc0, tc1]
                        final_src = {}
                        for step, sh in enumerate((1, 2, 4, 8)):
                            wnd = 2 * sh
                            act_parts = [(a_, b_) for (a_, b_, g) in parts if POOLW[g] >= wnd]
                            if not act_parts:
                                break
                            lo, hi = min(a_ for a_, b_ in act_parts), max(b_ for a_, b_ in act_parts)
                            s_in, s_out = srcs[step], srcs[step + 1]
                            st_ = wnd - 1
                            tkp_ = POOL.emit("tensor_tensor", out=s_out[lo:hi, st_:E], in0=s_in[lo:hi, st_:E],
                                             in1=s_in[lo:hi, st_ - sh:E - sh], op=ALU.add, deps=prev + [pers.get("pt_free")])
                            prev = [tkp_]
                            for (a_, b_, g) in parts:
                                if POOLW[g] == wnd:
                                    final_src[(a_, b_)] = s_out
                        tlast = None
                        for (a_, b_, g) in parts:
                            fs = final_src[(a_, b_)]
                            if ti == 0:
                                c0 = CH + HALO
                                tfix = DVE.emit("tensor_tensor", out=fs[a_:b_, c0:c0 + 16], in0=fs[a_:b_, c0:c0 + 16],
                                                in1=corr_sb[a_:b_, j * 16:j * 16 + 16], op=ALU.mult, deps=prev + [tkp])
                                prev = [tfix]
                            tlast = DVE.emit("scalar_tensor_tensor", out=dC[a_:b_, 12 + j, :T], in0=fs[a_:b_, CH:CH + T],
                                             scalar=pp[a_:b_, PP_OFF["invw"] + j:PP_OFF["invw"] + j + 1],
                                             in1=cbuf[a_:b_, CH:CH + T], op0=ALU.mult, op1=ALU.subtract, deps=prev + [ycat_free[0]])
                            prev = [tlast]
                        dC_tk[j] = tlast
                        pers["cbuf_free%d" % (j % 2)] = tlast
                        pers["pt_free"] = tlast
                        flush()
                defer(10, lnA_finish)
                for jo in (7, 8, 9):
                    c_out(jo)
                flush(force=True)
                cw_last = cw_state["last"]

                v_tk = [None] * 10
                lnB = {}

                def lnB_stats(j):
                    ln_stats(j, 10, onesB, v_tk[j], lnB)

                for j in range(10):
                    pk, pbuf, pdeps = PROJ.get()
                    tpe = slab_mm(pbuf[:, :T], 4096, hT_rhs, pdeps)
                    v_tk[j] = ACT.emit("activation", out=co[:, j, :T], in_=pbuf[:, :T], func=AF.Gelu_apprx_tanh,
                                       deps=[tpe, lnA["hn_tks"][j]])
                    PROJ.release(pk, v_tk[j])
                    defer(1, lambda j=j: lnB_stats(j))
                    flush()

                def vn_make(j, t5, T=T, l=l):
                    a = DVE.emit("tensor_tensor", out=co[:, j, :T], in0=co[:, j, :T], in1=rstd[:, :T], op=ALU.mult,
                                 deps=[t5, lnB["pe"]])
                    b = DVE.emit("tensor_tensor", out=co[:, j, :T], in0=co[:, j, :T], in1=nb[:, :T], op=ALU.add, deps=[a])
                    pers["nb_free"] = b
                    vb = vnb[j % 2]
                    c = ACT.emit("activation", out=vb[:, :T], in_=co[:, j, :T], func=AF.Identity,
                                 bias=ppc("blb", l * 10 + j), scale=ppc("blg", l * 10 + j),
                                 deps=[b, pers.get("vnb_free%d" % (j % 2))])
                    return c

                pw_last = None
                t5B = None
                vn_tk = [None] * 10
                for j in range(12):
                    pk, pbuf, pdeps = PROJ.get()
                    tpe = slab_mm(pbuf[:, :T], 4096, hT_rhs, pdeps)
                    flush()
                    if j == 0:
                        flush(force=True)
                        t5B = ln_finalize(lnB["pe"], T)
                        vn_tk[0] = vn_make(0, t5B)
                        vn_tk[1] = vn_make(1, t5B)
                    sk, sbuf_, sdeps = STG.get()
                    tsg = ACT.emit("activation", out=sbuf_[:, :T], in_=pbuf[:, :T], func=AF.Silu, deps=[tpe] + sdeps)
                    PROJ.release(pk, tsg)
                    ak, abuf, adeps = AUX.get()
                    tpw = slab_mm(abuf[:, :T], 1536, [hn[:, kc, :T] for kc in range(12)], [lnA["hn_tk"]] + adeps)
                    pw_last = tpw
                    ty = DVE.emit("tensor_tensor", out=ycat[:, j, :T], in0=abuf[:, :T], in1=sbuf_[:, :T], op=ALU.mult,
                                  deps=[tpw, tsg, ycat_free[0]])
                    yc_tk[j] = ty
                    AUX.release(ak, ty)
                    STG.release(sk, ty)
                    flush()
                pers["hn_free"] = pw_last

                hb = {}
                sp_state = {"last": None}

                def spatial_and_out(j, T=T, l=l, C=C):
                    sk, gu, tm, vt, tev = hb.pop(j)
                    ak, abuf, adeps = AUX.get()
                    tsp = None
                    for c in range(C):
                        tsp = PE.emit("matmul", out=abuf[:, c * 128:(c + 1) * 128], lhsT=vt[:, c * 128:(c + 1) * 128],
                                      rhs=wsT[:, l, j * 128:(j + 1) * 128], start=True, stop=True,
                                      deps=([tev, tk_ws] + adeps) if c == 0 else (), inc=(c == C - 1))
                    pers["vnT_free%d" % (j % 2)] = tsp
                    sp_state["last"] = tsp
                    sk3, tb, sdeps3 = STG.get()
                    bb = bbuf[j % 2]
                    SP.emit("dma_start", out=bb[:], in_=biasd[l, :, j * 128:(j + 1) * 128],
                            deps=[pers.get("bb_free%d" % (j % 2))], dma=btks[j])
                    tadd = None
                    for c in range(C):
                        tadd = DVE.emit("tensor_tensor", out=tb[:, c * 128:(c + 1) * 128], in0=abuf[:, c * 128:(c + 1) * 128],
                                        in1=bb[:], op=ALU.add,
                                        deps=([tsp, btks[j]] + sdeps3) if c == 0 else ())
                    pers["bb_free%d" % (j % 2)] = tadd
                    AUX.release(ak, tadd)
                    ty = DVE.emit("scalar_tensor_tensor", out=ycat[:, 12 + j, :T], in0=tb[:, :T], scalar=0.5, in1=gu[:, :T],
                                  op0=ALU.mult, op1=ALU.mult, deps=[tadd, tm, cw_last])
                    yc_tk[12 + j] = ty
                    STG.release(sk, ty)
                    STG.release(sk3, ty)

                for j in range(10):
                    pk, pbuf, pdeps = PROJ.get()
                    tpe = slab_mm(pbuf[:, :T], 4096, hT_rhs, pdeps)
                    sk, gu, sdeps = STG.get()
                    tgu = ACT.emit("activation", out=gu[:, :T], in_=pbuf[:, :T], func=AF.Gelu_apprx_tanh, deps=[tpe] + sdeps)
                    PROJ.release(pk, tgu)
                    if j >= 1:
                        spatial_and_out(j - 1)
                    pk2, pbuf2, pdeps2 = PROJ.get()
                    tpe2 = slab_mm(pbuf2[:, :T], 4096, hT_rhs, pdeps2)
                    sk2, th, sdeps2 = STG.get()
                    tth = ACT.emit("activation", out=th[:, :T], in_=pbuf2[:, :T], func=AF.Tanh, scale=0.5, deps=[tpe2] + sdeps2)
                    tsg = DVE.emit("scalar_tensor_tensor", out=th[:, :T], in0=th[:, :T], scalar=1.0, in1=pbuf2[:, :T],
                                   op0=ALU.add, op1=ALU.mult, deps=[tth])
                    PROJ.release(pk2, tsg)
                    tm = DVE.emit("tensor_tensor", out=gu[:, :T], in0=gu[:, :T], in1=th[:, :T], op=ALU.mult, deps=[tsg, tgu])
                    STG.release(sk2, tm)
                    vb = vnb[j % 2]
                    ttp = None
                    for c in range(C):
                        ttp = PE.emit("transpose", out=tps[:, c * 128:(c + 1) * 128], in_=vb[:, c * 128:(c + 1) * 128], identity=identb[:],
                                      deps=[vn_tk[j], tk0, pers.get("tps_free")] if c == 0 else (), inc=(c == C - 1))
                    pers["vnb_free%d" % (j % 2)] = ttp
                    vt = vnT[j % 2]
                    tev = DVE.emit("tensor_copy", out=vt[:, :T], in_=tps[:, :T], deps=[ttp, pers.get("vnT_free%d" % (j % 2))])
                    pers["tps_free"] = tev
                    hb[j] = (sk, gu, tm, vt, tev)
                    if j + 2 < 10:
                        vn_tk[j + 2] = vn_make(j + 2, t5B)
                    flush()
                spatial_and_out(9)
                hT_free[0] = ("pe", PE.count, s_pe)
                co_free = [sp_state["last"], lnB["pe"]]

                ycat_rhs = [ycat[:, kc, :T] for kc in range(32)]
                wo_last = None
                pend = None
                for n in range(32):
                    pk, pbuf, pdeps = PROJ.get()
                    tpe = slab_mm(pbuf[:, :T], 4096, ycat_rhs, (yc_tk if n == 0 else []) + pdeps)
                    wo_last = tpe
                    tx = DVE.emit("tensor_tensor", **dict(out=x[:, n, :T], in0=pbuf[:, :T], in1=x[:, n, :T], op=ALU.add),
                                  deps=[tpe, ht_tk])
                    PROJ.release(pk, tx)
                    x_tk[n] = tx
                    norm_accum(n, T, tx, nst)
                    if l == nlay - 1 and ti + 1 < len(tiles):
                        nt0, nT = tiles[ti + 1]
                        k, sbuf_, sdeps = STG.get()
                        pre_cnt[k] += 1
                        tld = ("s_pre%d" % k, 16 * pre_cnt[k], s_pre[k])
                        SP.emit("dma_start", out=sbuf_[:, :nT], in_=xT[n, :, nt0:nt0 + nT], deps=sdeps, dma=tld)
                        tsq = ACT.emit("activation", out=sbuf_[:, :nT], in_=sbuf_[:, :nT], func=AF.Square, deps=[tld])
                        if n == 0:
                            tacc = DVE.emit("tensor_copy", out=rstd[:, :nT], in_=sbuf_[:, :nT], deps=[tsq])
                        else:
                            tacc = DVE.emit("tensor_tensor", out=rstd[:, :nT], in0=rstd[:, :nT], in1=sbuf_[:, :nT], op=ALU.add, deps=[tsq])
                        STG.release(k, tacc)
                        pers["pre_tk"] = tacc
                    flush()
                ycat_free[0] = wo_last
                rs_tk = norm_finalize(T, nst)

            skip = HALO if ti == 0 else 0
            o0 = t0 + skip - HALO
            To = T - skip
            x_state["stores"] += 1
            for q in range(4):
                fin = None
                for n in range(8 * q, 8 * q + 8):
                    fin = DVE.emit("scalar_tensor_tensor", **dict(
                        out=x[:, n, :T], in0=x[:, n, :T], scalar=ppc("fg", n), in1=nrstd[:, :T], op0=ALU.mult, op1=ALU.mult),
                        deps=[rs_tk, x_tk[n]])
                tkq = ("s_o%d" % q, 16 * x_state["stores"], s_og[q])
                SP.emit("dma_start", **dict(
                    out=outT[8 * q:8 * q + 8, :, o0:o0 + To].rearrange("c p t -> p c t"), in_=x[:, 8 * q:8 * q + 8, skip:skip + To]),
                    deps=[fin], dma=tkq)
                x_state["store_tk"][q] = tkq

        assert W.consumed == len(plan), (W.consumed, len(plan))
        final_waits = list(x_state["store_tk"])

        with nc.Block() as block:
            @block.sync
            def _(e):
                SP.replay(e)
                for fw in final_waits:
                    e.wait_ge(fw[2], fw[1])

            @block.gpsimd
            def _(e):
                POOL.replay(e)

            @block.tensor
            def _(e):
                PE.replay(e)

            @block.scalar
            def _(e):
                ACT.replay(e)

            @block.vector
            def _(e):
                DVE.replay(e)
    return nc


def _prep_shared(norm_g, w_in, conv_w, conv_b, a_ln_g, a_ln_b, a_pw, b_ln_g, b_ln_b, b_ws, b_bias, c_w, c_scale, w_out, final_g):
    f = np.float32
    win = np.ascontiguousarray(np.asarray(w_in, f).reshape(L, 32, 128, 86, 128).transpose(0, 3, 2, 1, 4)).reshape(L, 86, 128, 4096)
    wpw = np.ascontiguousarray(np.asarray(a_pw, f).reshape(L, 12, 128, 12, 128).transpose(0, 3, 2, 1, 4)).reshape(L, 12, 128, 1536)
    wout = np.ascontiguousarray(np.asarray(w_out, f).reshape(L, 32, 128, 32, 128).transpose(0, 3, 2, 1, 4)).reshape(L, 32, 128, 4096)
    cwn = np.asarray(c_w, f)
    wbd = np.zeros((L, WC, WC), f)
    for g in range(4):
        wbd[:, g * 320:(g + 1) * 320, g * 320:(g + 1) * 320] = cwn[:, g]
    wcw = np.zeros((L, 10, 128, 640), f)
    for jo in range(10):
        for i, ji in enumerate(CW_JI[jo]):
            wcw[:, jo, :, i * 128:(i + 1) * 128] = wbd[:, ji * 128:(ji + 1) * 128, jo * 128:(jo + 1) * 128]
    wsr = np.ascontiguousarray(np.asarray(b_ws, f).transpose(0, 3, 1, 2)).reshape(L, 128, 1280)
    biasb = np.ascontiguousarray(np.broadcast_to(np.asarray(b_bias, f).reshape(L, 1, 1280), (L, 128, 1280)))
    pp = np.zeros((128, NPP), f)

    def put(name, arr):
        o = PP_OFF[name]
        pp[:, o:o + arr.shape[1]] = arr
    put("cw", np.asarray(conv_w, f).reshape(L, CONVW, 12, 128).transpose(3, 0, 2, 1).reshape(128, -1))
    for name, arr, nch in (("cb", conv_b, 12), ("alg", a_ln_g, 12), ("alb", a_ln_b, 12), ("blg", b_ln_g, 10),
                           ("blb", b_ln_b, 10), ("csc", c_scale, 10), ("ng", norm_g, 32)):
        put(name, np.asarray(arr, f).reshape(L, nch, 128).transpose(2, 0, 1).reshape(128, -1))
    put("fg", np.asarray(final_g, f).reshape(32, 128).T)
    ch = np.arange(WC)
    wnd = np.array(POOLW, f)[ch // 320]
    put("invw", (1.0 / wnd).astype(f).reshape(10, 128).T)
    return dict(win=win, wpw=wpw, wcw=wcw, wout=wout, wsr=wsr, biasb=biasb, pp=pp)


def _misc(seq_start):
    f = np.float32
    m = np.zeros((128, MISC_N), f)
    m[:, 0:128] = np.eye(128, dtype=f)
    q = np.arange(128)
    m[:, 128:256] = (q[:, None] <= q[None, :]).astype(f)
    ch = np.arange(WC)
    wnd = np.array(POOLW, f)[ch // 320]
    t = np.arange(16, dtype=f)
    if seq_start:
        corr = wnd[:, None] / np.minimum(t[None, :] + 1.0, wnd[:, None])
    else:
        corr = np.ones((WC, 16), f)
    m[:, 256:416] = corr.astype(f).reshape(10, 128, 16).transpose(1, 0, 2).reshape(128, 160)
    return m


_NC_CACHE = {}


def kernel(x, norm_g, w_in, conv_w, conv_b, a_ln_g, a_ln_b, a_pw, b_ln_g, b_ln_b, b_ws, b_bias, c_w, c_scale, w_out, final_g):
    x = np.asarray(x, np.float32)
    B, S, _ = x.shape
    shared = _prep_shared(norm_g, w_in, conv_w, conv_b, a_ln_g, a_ln_b, a_pw, b_ln_g, b_ln_b, b_ws, b_bias, c_w, c_scale, w_out, final_g)
    in_maps = []
    for c in range(NCORE):
        b, qd = divmod(c, NCORE // B)
        s0 = qd * TOK
        xt = np.zeros((LTOK, D), np.float32)
        if s0 > 0:
            xt[:] = x[b, s0 - HALO:s0 + TOK]
        else:
            xt[HALO:] = x[b, 0:TOK]
        m = dict(shared)
        m["xT"] = np.ascontiguousarray(xt.T).reshape(32, 128, LTOK)
        m["misc"] = _misc(s0 == 0)
        in_maps.append(m)
    if "nc" not in _NC_CACHE:
        _NC_CACHE["nc"] = build_nc()
    nc = _NC_CACHE["nc"]
    res = run_bass_kernel_spmd(nc, in_maps, core_ids=list(range(NCORE)))
    out = np.empty((B, S, D), np.float32)
    for c in range(NCORE):
        b, qd = divmod(c, NCORE // B)
        s0 = qd * TOK
        o = np.asarray(res.results[c]["outT"]).reshape(D, TOK)
        out[b, s0:s0 + TOK] = o.T
    return out
```

```python
import contextlib
import numpy as np
import concourse.bass as bass
import concourse.mybir as mybir
from concourse.bass_utils import run_bass_kernel_spmd

F32 = mybir.dt.float32
F32R = mybir.dt.float32r
BF16 = mybir.dt.bfloat16
AF = mybir.ActivationFunctionType
ALU = mybir.AluOpType

D = 4096
L = 2
WA, WB, WC = 1536, 1280, 1280
NCORE = 8
TOK = 2048
HALO = 128
LTOK = TOK + HALO
TMAX = 384
TILES = [(0, 384), (384, 384), (768, 384), (1152, 384), (1536, 384), (1920, 256)]
NW = 5
EPS = 1e-6
CONVW = 31
HH = 32
CH = 16
POOLW = (2, 4, 8, 16)

S_AVAL, S_AGLU, S_AGATE, S_BU, S_BV, S_BGATE, S_CIN, S_CGATE = 0, 12, 24, 36, 46, 56, 66, 76

def _chunk_groups(j):
    lo, hi = j * 128, j * 128 + 127
    return sorted(set([lo // 320, hi // 320]))
CW_JI = []
for _jo in range(10):
    gs = _chunk_groups(_jo)
    CW_JI.append([ji for ji in range(10) if set(_chunk_groups(ji)) & set(gs)])

def _pp_layout():
    off = {}
    o = 0
    for name, n in (("cw", L * 12 * CONVW), ("cb", L * 12), ("alg", L * 12), ("alb", L * 12),
                    ("blg", L * 10), ("blb", L * 10), ("csc", L * 10), ("ng", L * 32), ("fg", 32), ("invw", 10)):
        off[name] = o
        o += n
    return off, o
PP_OFF, NPP = _pp_layout()
PP_OFF["cwh"] = PP_OFF["cw"]
NX = 0
MISC_N = 128 + 128 + 160


class Stream:
    def __init__(self, name, sem, serial=False):
        self.name = name
        self.sem = sem
        self.serial = serial
        self.count = 0
        self.ops = []
        self.seen = {}

    def emit(self, _opname, deps=(), inc=True, dma=None, **kw):
        fn = (_opname, kw)
        waits = []
        deps = list(deps)
        if self.serial and self.count > 0 and dma is None:
            deps.append((self.name, self.count, self.sem))
        for d in deps:
            if d is None:
                continue
            key, val, sem = d
            if self.seen.get(key, 0) >= val:
                continue
            self.seen[key] = val
            waits.append((sem, val))
        tk = None
        if dma is not None:
            tk = dma
        elif inc:
            self.count += 1
            tk = (self.name, self.count, self.sem)
        self.ops.append((waits, fn, inc, dma))
        return tk

    def replay(self, eng):
        for waits, fn, inc, dma in self.ops:
            for sem, val in waits:
                eng.wait_ge(sem, val)
            ins = getattr(eng, fn[0])(**fn[1])
            if dma is not None:
                ins.then_inc(dma[2], 16)
            elif inc:
                ins.then_inc(self.sem, 1)


class Ring:
    def __init__(self, bufs):
        self.bufs = bufs
        self.i = 0
        self.rel = [[] for _ in bufs]

    def get(self):
        k = self.i % len(self.bufs)
        self.i += 1
        d = self.rel[k]
        self.rel[k] = []
        return k, self.bufs[k], d

    def release(self, k, *tickets):
        self.rel[k].extend(t for t in tickets if t is not None)


def build_nc(tiles=None, nlay=L, ltok=LTOK):
    tiles = TILES if tiles is None else tiles
    nc = bass.Bass("TRN2", target_bir_lowering=False)
    xT = nc.dram_tensor("xT", [32, 128, ltok], F32, kind="ExternalInput").ap()
    win = nc.dram_tensor("win", [L, 86, 128, 4096], F32, kind="ExternalInput").ap()
    wpw = nc.dram_tensor("wpw", [L, 12, 128, 1536], F32, kind="ExternalInput").ap()
    wcw = nc.dram_tensor("wcw", [L, 10, 128, 640], F32, kind="ExternalInput").ap()
    wout = nc.dram_tensor("wout", [L, 32, 128, 4096], F32, kind="ExternalInput").ap()
    wsr = nc.dram_tensor("wsr", [L, 128, 1280], F32, kind="ExternalInput").ap()
    ppd = nc.dram_tensor("pp", [128, NPP], F32, kind="ExternalInput").ap()
    biasd = nc.dram_tensor("biasb", [L, 128, 1280], F32, kind="ExternalInput").ap()
    miscd = nc.dram_tensor("misc", [128, MISC_N], F32, kind="ExternalInput").ap()
    outT = nc.dram_tensor("outT", [32, 128, ltok - HALO], F32, kind="ExternalOutput").ap()

    es = contextlib.ExitStack()
    with es:
        def sb(name, shape, dt):
            return es.enter_context(nc.sbuf_tensor(name, shape, dt))

        def ps(name, shape, dt=F32):
            return es.enter_context(nc.psum_tensor(name, shape, dt))

        def sem(name):
            return es.enter_context(nc.semaphore(name))

        x = sb("x", [128, 32, TMAX], F32)
        hT = sb("hT", [128, 32, TMAX], BF16)
        ycat = sb("ycat", [128, 32, TMAX], BF16)
        wb = [sb("w%d" % i, [128, 4096], BF16) for i in range(NW)]
        co = sb("co", [128, 12, TMAX], F32)
        hn = sb("hn", [128, 12, TMAX], BF16)
        hbuf = sb("hbuf", [128, HH + TMAX], F32)
        cbufs = [sb("cbuf%d" % i, [128, CH + TMAX], F32) for i in range(2)]
        ptA = sb("ptA", [128, CH + TMAX], F32)
        ptB = sb("ptB", [128, CH + TMAX], F32)
        stg = [sb("stg%d" % i, [128, TMAX], F32) for i in range(7)]
        rstd = sb("rstd", [128, TMAX], F32)
        nb = sb("nb", [128, TMAX], F32)
        nacc = sb("nacc", [128, TMAX], F32)
        mean = nacc
        nrstd = sb("nrstd", [128, TMAX], F32)
        vnb = [sb("vnb%d" % i, [128, TMAX], BF16) for i in range(2)]
        vnT = [sb("vnT%d" % i, [128, TMAX], BF16) for i in range(2)]
        htail = sb("htail", [128, L, 12, HH], F32)
        ctail = sb("ctail", [128, L, 10, CH], F32)
        wsT = sb("wsT", [128, L, 1280], BF16)
        bbuf = [sb("bb%d" % i, [128, 128], F32) for i in range(2)]
        pp = sb("pp_sb", [128, NPP + NX], F32)
        corr_sb = sb("corr_sb", [128, 160], F32)
        identb = sb("identb", [128, 128], BF16)
        onesA = sb("onesA", [128, 128], F32)
        onesB = sb("onesB", [128, 128], F32)
        onesN = sb("onesN", [128, 128], F32)

        print('sbuf bytes remaining', nc.sbuf_bytes_remaining)
        dC = ycat

        proj = [ps("pj%d" % i, [128, 512]) for i in range(3)]
        aux = [ps("ax%d" % i, [128, 512]) for i in range(2)]
        st0 = ps("st0", [128, 512])
        st1 = ps("st1", [128, 512])
        tps = ps("tps", [128, 1024], BF16)

        s_pe, s_act, s_dve, s_pool = sem("s_pe"), sem("s_act"), sem("s_dve"), sem("s_pool")
        s_w = [sem("s_w%d" % i) for i in range(NW)]
        s_p = sem("s_p")
        s_bb = [sem("s_bb%d" % i) for i in range(2)]
        bb_cnt = [0, 0]
        s_pre = [sem("s_pre%d" % i) for i in range(7)]
        pre_cnt = [0] * 7
        s_xg = [sem("s_x%d" % i) for i in range(4)]
        s_og = [sem("s_o%d" % i) for i in range(4)]

        PE = Stream("pe", s_pe)
        ACT = Stream("act", s_act, serial=True)
        DVE = Stream("dve", s_dve, serial=True)
        POOL = Stream("pool", s_pool, serial=True)
        SP = Stream("sp", None)

        PROJ = Ring(proj)
        AUX = Ring(aux)
        STG = Ring(stg)

        def ppc(name, idx):
            o = PP_OFF[name] + idx
            return pp[:, o:o + 1]

        plan = []

        def plan_cout(l, jo):
            plan.append((win[l, S_CGATE + jo], 4096))
            plan.append((wcw[l, jo, :, 0:len(CW_JI[jo]) * 128], len(CW_JI[jo]) * 128))

        for ti in range(len(tiles)):
            for l in range(nlay):
                for j in range(12):
                    if j >= 5:
                        plan_cout(l, j - 5)
                    plan.append((win[l, S_AGLU + j], 4096))
                    plan.append((win[l, S_AVAL + j], 4096))
                    if j < 10:
                        plan.append((win[l, S_CIN + j], 4096))
                for jo in (7, 8, 9):
                    plan_cout(l, jo)
                for j in range(10):
                    plan.append((win[l, S_BV + j], 4096))
                for j in range(12):
                    plan.append((win[l, S_AGATE + j], 4096))
                    plan.append((wpw[l, j], 1536))
                for j in range(10):
                    plan.append((win[l, S_BU + j], 4096))
                    plan.append((win[l, S_BGATE + j], 4096))
                for n in range(32):
                    plan.append((wout[l, n], 4096))

        class WMgr:
            def __init__(self):
                self.issued = 0
                self.consumed = 0
                self.free_tk = [None] * NW
                self.fill = [0] * NW
                self.load_tk = {}

            def issue_upto(self, n):
                while self.issued < min(n, len(plan)):
                    k = self.issued
                    b = k % NW
                    src, ncols = plan[k]
                    self.fill[b] += 1
                    tk = ("w%d" % b, 16 * self.fill[b], s_w[b])
                    dst = wb[b][:, 0:ncols]
                    POOL.emit("dma_start", **dict(out=dst, in_=src, max_dma_last_dim=8192),
                              deps=[self.free_tk[b]], dma=tk)
                    self.load_tk[k] = tk
                    self.issued += 1

            def take(self, ncols):
                k = self.consumed
                self.issue_upto(k + 1)
                assert plan[k][1] == ncols, (k, plan[k][1], ncols)
                return k, wb[k % NW], self.load_tk.pop(k)

            def done(self, k, tk):
                self.free_tk[k % NW] = tk
                self.consumed += 1
                self.issue_upto(k + NW + 1)

        W = WMgr()

        deferred = []
        slab_ctr = [0]

        def defer(delay, fn):
            deferred.append((slab_ctr[0] + delay, fn))

        def flush(force=False):
            while deferred and (force or deferred[0][0] <= slab_ctr[0]):
                _, fn = deferred.pop(0)
                fn()

        def slab_mm(out_ap, ncols, rhs_list, deps, deps_i=None):
            k, buf, ltk = W.take(ncols)
            n = len(rhs_list)
            assert n * 128 == ncols
            tk = None
            for i, rhs in enumerate(rhs_list):
                tk = PE.emit("matmul", **dict(out=out_ap, lhsT=buf[:, i * 128:(i + 1) * 128], rhs=rhs,
                                                                     start=(i == 0), stop=(i == n - 1)),
                             deps=(([ltk] + list(deps)) if i == 0 else []) + ([deps_i[i]] if deps_i else []), inc=(i == n - 1))
            W.done(k, tk)
            slab_ctr[0] += 1
            return tk

        tkp = ("s_p", 16 * 5, s_p)
        SP.emit("dma_start", **dict(out=pp[:, 0:NPP], in_=ppd), dma=("s_p", 16, s_p))
        SP.emit("dma_start", **dict(out=stg[0][:, 0:256], in_=miscd[:, 0:256]), dma=("s_p", 32, s_p))
        SP.emit("dma_start", **dict(out=corr_sb[:], in_=miscd[:, 256:416]), dma=("s_p", 48, s_p))
        costage = co[:, 0:8, 0:320]
        SP.emit("dma_start", **dict(out=co[:, 0:4, 0:320], in_=wsr[0].rearrange("q (a b) -> q a b", a=4)), dma=("s_p", 64, s_p))
        SP.emit("dma_start", **dict(out=co[:, 4:8, 0:320], in_=wsr[1].rearrange("q (a b) -> q a b", a=4)), dma=tkp)
        W.issue_upto(NW)
        tk0 = DVE.emit("tensor_copy", **dict(out=identb[:], in_=stg[0][:, 0:128]), deps=[tkp])
        DVE.emit("memset", **dict(ap=onesA[:], constant=1.0 / WA))
        DVE.emit("memset", **dict(ap=onesB[:], constant=1.0 / WB))
        DVE.emit("memset", **dict(ap=onesN[:], constant=1.0 / D))
        DVE.emit("memset", **dict(ap=htail[:], constant=0.0))
        tk_init = DVE.emit("memset", **dict(ap=ctail[:], constant=0.0))
        tk_half = DVE.emit("tensor_scalar", out=pp[:, PP_OFF["cw"]:PP_OFF["cw"] + L * 12 * CONVW],
                           in0=pp[:, PP_OFF["cw"]:PP_OFF["cw"] + L * 12 * CONVW], scalar1=0.5, scalar2=None, op0=ALU.mult, deps=[tkp])
        tk_ws = None
        for l in range(L):
            for h in range(10):
                a, r = divmod(h * 128, 320)
                for p0 in range(0, 128, 64):
                    col = h * 128 + p0
                    a, r = divmod(col, 320)
                    tk_ws = DVE.emit("tensor_tensor", **dict(
                        out=wsT[:, l, col:col + 64], in0=co[:, 4 * l + a, r:r + 64], in1=stg[0][:, 128 + p0:128 + p0 + 64], op=ALU.mult),
                        deps=[tkp])
        STG.release(0, tk_ws)
        co_free = [tk_ws]
        hT_free = [None]
        ycat_free = [None]
        stat_free = [[], []]
        htail_tk = [[tk_init] * 12 for _ in range(L)]
        ctail_tk = [[tk_init] * 10 for _ in range(L)]
        x_state = {"loads": 0, "stores": 0, "store_tk": [None] * 4}

        def norm_accum(n, T, xtk, st):
            k, sbuf_, sdeps = STG.get()
            tsq = ACT.emit("activation", **dict(out=sbuf_[:, :T], in_=x[:, n, :T], func=AF.Square), deps=[xtk] + sdeps)
            if n == 0:
                t = DVE.emit("tensor_copy", **dict(out=nacc[:, :T], in_=sbuf_[:, :T]), deps=[tsq] + st["nacc_free"])
            else:
                t = DVE.emit("tensor_tensor", **dict(out=nacc[:, :T], in0=nacc[:, :T], in1=sbuf_[:, :T], op=ALU.add),
                             deps=[tsq, st["nacc_tk"]])
            st["nacc_tk"] = t
            STG.release(k, t)

        def norm_finalize(T, st, acc=None, acc_tk=None):
            if acc is None:
                acc, acc_tk = nacc, st["nacc_tk"]
            tpe = PE.emit("matmul", **dict(out=st0[:, :T], lhsT=onesN[:], rhs=acc[:, :T], start=True, stop=True),
                          deps=[acc_tk] + stat_free[0])
            stat_free[0] = []
            if acc is nacc:
                st["nacc_free"] = [tpe]
            ta = ACT.emit("activation", out=nrstd[:, :T], in_=st0[:, :T], func=AF.Sqrt, bias=EPS, scale=1.0,
                          deps=[tpe, st.get("nrstd_free")])
            stat_free[0].append(ta)
            td = DVE.emit("reciprocal", out=nrstd[:, :T], in_=nrstd[:, :T], deps=[ta])
            return td

        nst = {"nacc_free": [], "nacc_tk": None}
        pers = {}

        for ti, (t0, T) in enumerate(tiles):
            C = T // 128
            x_state["loads"] += 1
            xg_tk = []
            for q in range(4):
                tkq = ("s_x%d" % q, 16 * x_state["loads"], s_xg[q])
                SP.emit("dma_start", **dict(out=x[:, 8 * q:8 * q + 8, 0:T],
                                            in_=xT[8 * q:8 * q + 8, :, t0:t0 + T].rearrange("c p t -> p c t")),
                        deps=[x_state["store_tk"][q]], dma=tkq)
                xg_tk.append(tkq)
            if pers.get("pre_tk") is not None:
                rs_tk = norm_finalize(T, nst, acc=rstd, acc_tk=pers["pre_tk"])
                pers["pre_tk"] = None
            else:
                for n in range(32):
                    norm_accum(n, T, xg_tk[n // 8], nst)
                rs_tk = norm_finalize(T, nst)
            x_tk = [xg_tk[n // 8] for n in range(32)]

            for l in range(nlay):
                btks = []
                for j in range(10):
                    bb_cnt[j % 2] += 1
                    btk_j = ("s_bb%d" % (j % 2), 16 * bb_cnt[j % 2], s_bb[j % 2])
                    btks.append(btk_j)

                ht_tk = None
                ht_tks = []
                for n in range(32):
                    ht_tk = DVE.emit("scalar_tensor_tensor", **dict(
                        out=hT[:, n, :T], in0=x[:, n, :T], scalar=ppc("ng", l * 32 + n), in1=nrstd[:, :T],
                        op0=ALU.mult, op1=ALU.mult), deps=[rs_tk, x_tk[n], hT_free[0], tkp])
                    ht_tks.append(ht_tk)
                nst["nrstd_free"] = ht_tk
                hT_rhs = [hT[:, kc, :T] for kc in range(32)]

                conv_tk = [None] * 12
                dC_tk = [None] * 10
                lnA = {}

                def ln_stats(j, nch, ones, src_tk, holder, T=T):
                    k, sbuf_, sdeps = STG.get()
                    tsq = ACT.emit("activation", out=sbuf_[:, :T], in_=co[:, j, :T], func=AF.Square, deps=[src_tk] + sdeps)
                    if j == 0:
                        a1 = POOL.emit("tensor_copy", out=nacc[:, :T], in_=co[:, j, :T],
                                       deps=[src_tk, pers.get("mean_free")] + nst["nacc_free"])
                        a2 = POOL.emit("tensor_copy", out=nb[:, :T], in_=sbuf_[:, :T], deps=[tsq, pers.get("nb_free")])
                    else:
                        a1 = POOL.emit("tensor_tensor", out=nacc[:, :T], in0=nacc[:, :T], in1=co[:, j, :T], op=ALU.add, deps=[src_tk])
                        a2 = POOL.emit("tensor_tensor", out=nb[:, :T], in0=nb[:, :T], in1=sbuf_[:, :T], op=ALU.add, deps=[tsq])
                    STG.release(k, a2)
                    if j == nch - 1:
                        d0, d1 = stat_free[0], stat_free[1]
                        stat_free[0] = []
                        stat_free[1] = []
                        PE.emit("matmul", out=st0[:, :T], lhsT=ones[:], rhs=nacc[:, :T], start=True, stop=True, deps=[a1] + d0, inc=False)
                        te = PE.emit("matmul", out=st1[:, :T], lhsT=ones[:], rhs=nb[:, :T], start=True, stop=True, deps=[a2] + d1)
                        holder["pe"] = te
                        nst["nacc_free"] = [te]

                def lnA_stats(j):
                    ln_stats(j, 12, onesA, conv_tk[j], lnA)

                def ln_finalize(pe_tk, T):
                    t1 = DVE.emit("tensor_copy", **dict(out=mean[:, :T], in_=st0[:, :T]), deps=[pe_tk] + nst["nacc_free"])
                    t2 = DVE.emit("tensor_tensor", out=rstd[:, :T], in0=mean[:, :T], in1=mean[:, :T], op=ALU.mult, deps=[t1])
                    t3 = DVE.emit("tensor_tensor", out=rstd[:, :T], in0=st1[:, :T], in1=rstd[:, :T], op=ALU.subtract, deps=[t2])
                    t3b = DVE.emit("tensor_scalar", out=rstd[:, :T], in0=rstd[:, :T], scalar1=0.0, scalar2=None, op0=ALU.max, deps=[t3])
                    ta = ACT.emit("activation", out=rstd[:, :T], in_=rstd[:, :T], func=AF.Sqrt, bias=EPS, scale=1.0, deps=[t3b])
                    t4 = DVE.emit("reciprocal", **dict(out=rstd[:, :T], in_=rstd[:, :T]), deps=[ta])
                    t5 = DVE.emit("scalar_tensor_tensor", **dict(out=nb[:, :T], in0=mean[:, :T], scalar=-1.0, in1=rstd[:, :T],
                                                                   op0=ALU.mult, op1=ALU.mult), deps=[t4])
                    stat_free[0].append(t1)
                    stat_free[1].append(t3)
                    pers["mean_free"] = t5
                    return t5

                def lnA_finish(T=T, l=l):
                    t5 = ln_finalize(lnA["pe"], T)
                    last = None
                    for j in range(12):
                        a = DVE.emit("tensor_tensor", **dict(out=co[:, j, :T], in0=co[:, j, :T], in1=rstd[:, :T], op=ALU.mult),
                                     deps=[t5, lnA["pe"]])
                        b = DVE.emit("tensor_tensor", **dict(out=co[:, j, :T], in0=co[:, j, :T], in1=nb[:, :T], op=ALU.add),
                                     deps=[a])
                        pers["nb_free"] = b
                        last = lnA.setdefault("hn_tks", [None] * 12)[j] = ACT.emit("activation", **dict(out=hn[:, j, :T], in_=co[:, j, :T], func=AF.Silu,
                                                                    bias=ppc("alb", l * 12 + j), scale=ppc("alg", l * 12 + j)),
                                        deps=[b, pers.get("hn_free")])
                    lnA["hn_tk"] = last

                cw_state = {"last": None}
                yc_tk = [None] * 32

                def c_out(jo, T=T, l=l):
                    pk, pbuf, pdeps = PROJ.get()
                    tpe = slab_mm(pbuf[:, :T], 4096, hT_rhs, pdeps)
                    sk, sbuf_, sdeps = STG.get()
                    tsg = ACT.emit("activation", out=sbuf_[:, :T], in_=pbuf[:, :T], func=AF.Silu, deps=[tpe] + sdeps)
                    PROJ.release(pk, tsg)
                    ak, abuf, adeps = AUX.get()
                    jis = CW_JI[jo]
                    tcw = slab_mm(abuf[:, :T], len(jis) * 128, [dC[:, 12 + ji, :T] for ji in jis],
                                  [dC_tk[ji] for ji in jis] + adeps)
                    cw_state["last"] = tcw
                    ty = DVE.emit("scalar_tensor_tensor", out=ycat[:, 22 + jo, :T], in0=abuf[:, :T], scalar=ppc("csc", l * 10 + jo),
                                  in1=sbuf_[:, :T], op0=ALU.mult, op1=ALU.mult, deps=[tcw, tsg, ycat_free[0], tkp])
                    yc_tk[22 + jo] = ty
                    AUX.release(ak, ty)
                    STG.release(sk, ty)
                    flush()

                for j in range(12):
                    if j >= 5:
                        c_out(j - 5)
                    pk, pbuf, pdeps = PROJ.get()
                    tpe = slab_mm(pbuf[:, :T], 4096, hT_rhs, pdeps, deps_i=(ht_tks if j == 0 else None))
                    sk, sbuf_, sdeps = STG.get()
                    tsig = ACT.emit("activation", out=sbuf_[:, :T], in_=pbuf[:, :T], func=AF.Tanh, scale=0.5, deps=[tpe] + sdeps)
                    PROJ.release(pk, tsig)
                    flush()
                    pk2, pbuf2, pdeps2 = PROJ.get()
                    tpe2 = slab_mm(pbuf2[:, :T], 4096, hT_rhs, pdeps2)
                    skv, vbuf_, sdepsv = STG.get()
                    tval = ACT.emit("activation", out=vbuf_[:, :T], in_=pbuf2[:, :T], func=AF.Copy, deps=[tpe2] + sdepsv)
                    PROJ.release(pk2, tval)
                    th0 = DVE.emit("tensor_copy", out=hbuf[:, 0:HH], in_=htail[:, l, j, :], deps=[htail_tk[l][j]])
                    tglu = DVE.emit("scalar_tensor_tensor", out=hbuf[:, HH:HH + T], in0=sbuf_[:, :T], scalar=1.0, in1=vbuf_[:, :T],
                                    op0=ALU.add, op1=ALU.mult, deps=[tval, tsig])
                    STG.release(sk, tglu)
                    STG.release(skv, tglu)
                    htail_tk[l][j] = DVE.emit("tensor_copy", out=htail[:, l, j, :], in_=hbuf[:, T:T + HH], deps=[tglu])
                    base = (l * 12 + j) * CONVW
                    tk = DVE.emit("tensor_scalar", out=co[:, j, :T], in0=hbuf[:, 2:2 + T], scalar1=ppc("cwh", base),
                                  scalar2=ppc("cb", l * 12 + j), op0=ALU.mult, op1=ALU.add, deps=[th0, tglu, tk_half] + co_free)
                    co_free = []
                    for k in range(1, CONVW):
                        tk = DVE.emit("scalar_tensor_tensor", out=co[:, j, :T], in0=hbuf[:, 2 + k:2 + k + T], scalar=ppc("cwh", base + k),
                                      in1=co[:, j, :T], op0=ALU.mult, op1=ALU.add, deps=[tk])
                    conv_tk[j] = tk
                    defer(9, lambda j=j: lnA_stats(j))
                    flush()
                    if j < 10:
                        pk3, pbuf3, pdeps3 = PROJ.get()
                        tpe3 = slab_mm(pbuf3[:, :T], 4096, hT_rhs, pdeps3)
                        cbuf = cbufs[j % 2]
                        tc1 = ACT.emit("activation", out=cbuf[:, CH:CH + T], in_=pbuf3[:, :T], func=AF.Copy,
                                       deps=[tpe3, pers.get("cbuf_free%d" % (j % 2))])
                        PROJ.release(pk3, tc1)
                        tc0 = POOL.emit("tensor_copy", out=cbuf[:, 0:CH], in_=ctail[:, l, j, :],
                                        deps=[ctail_tk[l][j], pers.get("cbuf_free%d" % (j % 2))])
                        ctail_tk[l][j] = POOL.emit("tensor_copy", out=ctail[:, l, j, :], in_=cbuf[:, T:T + CH], deps=[tc1])
                        gs = _chunk_groups(j)
                        parts = [(0, 128, gs[0])] if len(gs) == 1 else [(0, 64, gs[0]), (64, 128, gs[1])]
                        E = CH + T
                        srcs = [cbuf, ptA, ptB, ptA, ptB]
                        prev = [tc0, tc1]
                        final_src = {}
                        for step, sh in enumerate((1, 2, 4, 8)):
                            wnd = 2 * sh
                            act_parts = [(a_, b_) for (a_, b_, g) in parts if POOLW[g] >= wnd]
                            if not act_parts:
                                break
                            lo, hi = min(a_ for a_, b_ in act_parts), max(b_ for a_, b_ in act_parts)
                            s_in, s_out = srcs[step], srcs[step + 1]
                            st_ = wnd - 1
                            tkp_ = POOL.emit("tensor_tensor", out=s_out[lo:hi, st_:E], in0=s_in[lo:hi, st_:E],
                                             in1=s_in[lo:hi, st_ - sh:E - sh], op=ALU.add, deps=prev + [pers.get("pt_free")])
                            prev = [tkp_]
                            for (a_, b_, g) in parts:
                                if POOLW[g] == wnd:
                                    final_src[(a_, b_)] = s_out
                        tlast = None
                        for (a_, b_, g) in parts:
                            fs = final_src[(a_, b_)]
                            if ti == 0:
                                c0 = CH + HALO
                                tfix = DVE.emit("tensor_tensor", out=fs[a_:b_, c0:c0 + 16], in0=fs[a_:b_, c0:c0 + 16],
                                                in1=corr_sb[a_:b_, j * 16:j * 16 + 16], op=ALU.mult, deps=prev + [tkp])
                                prev = [tfix]
                            tlast = DVE.emit("scalar_tensor_tensor", out=dC[a_:b_, 12 + j, :T], in0=fs[a_:b_, CH:CH + T],
                                             scalar=pp[a_:b_, PP_OFF["invw"] + j:PP_OFF["invw"] + j + 1],
                                             in1=cbuf[a_:b_, CH:CH + T], op0=ALU.mult, op1=ALU.subtract, deps=prev + [ycat_free[0]])
                            prev = [tlast]
                        dC_tk[j] = tlast
                        pers["cbuf_free%d" % (j % 2)] = tlast
                        pers["pt_free"] = tlast
                        flush()
                defer(10, lnA_finish)
                for jo in (7, 8, 9):
                    c_out(jo)
                flush(force=True)
                cw_last = cw_state["last"]

                v_tk = [None] * 10
                lnB = {}

                def lnB_stats(j):
                    ln_stats(j, 10, onesB, v_tk[j], lnB)

                for j in range(10):
                    pk, pbuf, pdeps = PROJ.get()
                    tpe = slab_mm(pbuf[:, :T], 4096, hT_rhs, pdeps)
                    v_tk[j] = ACT.emit("activation", out=co[:, j, :T], in_=pbuf[:, :T], func=AF.Gelu_apprx_tanh,
                                       deps=[tpe, lnA["hn_tks"][j]])
                    PROJ.release(pk, v_tk[j])
                    defer(1, lambda j=j: lnB_stats(j))
                    flush()

                def vn_make(j, t5, T=T, l=l):
                    a = DVE.emit("tensor_tensor", out=co[:, j, :T], in0=co[:, j, :T], in1=rstd[:, :T], op=ALU.mult,
                                 deps=[t5, lnB["pe"]])
                    b = DVE.emit("tensor_tensor", out=co[:, j, :T], in0=co[:, j, :T], in1=nb[:, :T], op=ALU.add, deps=[a])
                    pers["nb_free"] = b
                    vb = vnb[j % 2]
                    c = ACT.emit("activation", out=vb[:, :T], in_=co[:, j, :T], func=AF.Identity,
                                 bias=ppc("blb", l * 10 + j), scale=ppc("blg", l * 10 + j),
                                 deps=[b, pers.get("vnb_free%d" % (j % 2))])
                    return c

                pw_last = None
                t5B = None
                vn_tk = [None] * 10
                for j in range(12):
                    pk, pbuf, pdeps = PROJ.get()
                    tpe = slab_mm(pbuf[:, :T], 4096, hT_rhs, pdeps)
                    flush()
                    if j == 0:
                        flush(force=True)
                        t5B = ln_finalize(lnB["pe"], T)
                        vn_tk[0] = vn_make(0, t5B)
                        vn_tk[1] = vn_make(1, t5B)
                    sk, sbuf_, sdeps = STG.get()
                    tsg = ACT.emit("activation", out=sbuf_[:, :T], in_=pbuf[:, :T], func=AF.Silu, deps=[tpe] + sdeps)
                    PROJ.release(pk, tsg)
                    ak, abuf, adeps = AUX.get()
                    tpw = slab_mm(abuf[:, :T], 1536, [hn[:, kc, :T] for kc in range(12)], [lnA["hn_tk"]] + adeps)
                    pw_last = tpw
                    ty = DVE.emit("tensor_tensor", out=ycat[:, j, :T], in0=abuf[:, :T], in1=sbuf_[:, :T], op=ALU.mult,
                                  deps=[tpw, tsg, ycat_free[0]])
                    yc_tk[j] = ty
                    AUX.release(ak, ty)
                    STG.release(sk, ty)
                    flush()
                pers["hn_free"] = pw_last

                hb = {}
                sp_state = {"last": None}

                def spatial_and_out(j, T=T, l=l, C=C):
                    sk, gu, tm, vt, tev = hb.pop(j)
                    ak, abuf, adeps = AUX.get()
                    tsp = None
                    for c in range(C):
                        tsp = PE.emit("matmul", out=abuf[:, c * 128:(c + 1) * 128], lhsT=vt[:, c * 128:(c + 1) * 128],
                                      rhs=wsT[:, l, j * 128:(j + 1) * 128], start=True, stop=True,
                                      deps=([tev, tk_ws] + adeps) if c == 0 else (), inc=(c == C - 1))
                    pers["vnT_free%d" % (j % 2)] = tsp
                    sp_state["last"] = tsp
                    sk3, tb, sdeps3 = STG.get()
                    bb = bbuf[j % 2]
                    SP.emit("dma_start", out=bb[:], in_=biasd[l, :, j * 128:(j + 1) * 128],
                            deps=[pers.get("bb_free%d" % (j % 2))], dma=btks[j])
                    tadd = None
                    for c in range(C):
                        tadd = DVE.emit("tensor_tensor", out=tb[:, c * 128:(c + 1) * 128], in0=abuf[:, c * 128:(c + 1) * 128],
                                        in1=bb[:], op=ALU.add,
                                        deps=([tsp, btks[j]] + sdeps3) if c == 0 else ())
                    pers["bb_free%d" % (j % 2)] = tadd
                    AUX.release(ak, tadd)
                    ty = DVE.emit("scalar_tensor_tensor", out=ycat[:, 12 + j, :T], in0=tb[:, :T], scalar=0.5, in1=gu[:, :T],
                                  op0=ALU.mult, op1=ALU.mult, deps=[tadd, tm, cw_last])
                    yc_tk[12 + j] = ty
                    STG.release(sk, ty)
                    STG.release(sk3, ty)

                for j in range(10):
                    pk, pbuf, pdeps = PROJ.get()
                    tpe = slab_mm(pbuf[:, :T], 4096, hT_rhs, pdeps)
                    sk, gu, sdeps = STG.get()
                    tgu = ACT.emit("activation", out=gu[:, :T], in_=pbuf[:, :T], func=AF.Gelu_apprx_tanh, deps=[tpe] + sdeps)
                    PROJ.release(pk, tgu)
                    if j >= 1:
                        spatial_and_out(j - 1)
                    pk2, pbuf2, pdeps2 = PROJ.get()
                    tpe2 = slab_mm(pbuf2[:, :T], 4096, hT_rhs, pdeps2)
                    sk2, th, sdeps2 = STG.get()
                    tth = ACT.emit("activation", out=th[:, :T], in_=pbuf2[:, :T], func=AF.Tanh, scale=0.5, deps=[tpe2] + sdeps2)
                    tsg = DVE.emit("scalar_tensor_tensor", out=th[:, :T], in0=th[:, :T], scalar=1.0, in1=pbuf2[:, :T],
                                   op0=ALU.add, op1=ALU.mult, deps=[tth])
                    PROJ.release(pk2, tsg)
                    tm = DVE.emit("tensor_tensor", out=gu[:, :T], in0=gu[:, :T], in1=th[:, :T], op=ALU.mult, deps=[tsg, tgu])
                    STG.release(sk2, tm)
                    vb = vnb[j % 2]
                    ttp = None
                    for c in range(C):
                        ttp = PE.emit("transpose", out=tps[:, c * 128:(c + 1) * 128], in_=vb[:, c * 128:(c + 1) * 128], identity=identb[:],
                                      deps=[vn_tk[j], tk0, pers.get("tps_free")] if c == 0 else (), inc=(c == C - 1))
                    pers["vnb_free%d" % (j % 2)] = ttp
                    vt = vnT[j % 2]
                    tev = DVE.emit("tensor_copy", out=vt[:, :T], in_=tps[:, :T], deps=[ttp, pers.get("vnT_free%d" % (j % 2))])
                    pers["tps_free"] = tev
                    hb[j] = (sk, gu, tm, vt, tev)
                    if j + 2 < 10:
                        vn_tk[j + 2] = vn_make(j + 2, t5B)
                    flush()
                spatial_and_out(9)
                hT_free[0] = ("pe", PE.count, s_pe)
                co_free = [sp_state["last"], lnB["pe"]]

                ycat_rhs = [ycat[:, kc, :T] for kc in range(32)]
                wo_last = None
                pend = None
                for n in range(32):
                    pk, pbuf, pdeps = PROJ.get()
                    tpe = slab_mm(pbuf[:, :T], 4096, ycat_rhs, (yc_tk if n == 0 else []) + pdeps)
                    wo_last = tpe
                    tx = DVE.emit("tensor_tensor", **dict(out=x[:, n, :T], in0=pbuf[:, :T], in1=x[:, n, :T], op=ALU.add),
                                  deps=[tpe, ht_tk])
                    PROJ.release(pk, tx)
                    x_tk[n] = tx
                    norm_accum(n, T, tx, nst)
                    if l == nlay - 1 and ti + 1 < len(tiles):
                        nt0, nT = tiles[ti + 1]
                        k, sbuf_, sdeps = STG.get()
                        pre_cnt[k] += 1
                        tld = ("s_pre%d" % k, 16 * pre_cnt[k], s_pre[k])
                        SP.emit("dma_start", out=sbuf_[:, :nT], in_=xT[n, :, nt0:nt0 + nT], deps=sdeps, dma=tld)
                        tsq = ACT.emit("activation", out=sbuf_[:, :nT], in_=sbuf_[:, :nT], func=AF.Square, deps=[tld])
                        if n == 0:
                            tacc = DVE.emit("tensor_copy", out=rstd[:, :nT], in_=sbuf_[:, :nT], deps=[tsq])
                        else:
                            tacc = DVE.emit("tensor_tensor", out=rstd[:, :nT], in0=rstd[:, :nT], in1=sbuf_[:, :nT], op=ALU.add, deps=[tsq])
                        STG.release(k, tacc)
                        pers["pre_tk"] = tacc
                    flush()
                ycat_free[0] = wo_last
                rs_tk = norm_finalize(T, nst)

            skip = HALO if ti == 0 else 0
            o0 = t0 + skip - HALO
            To = T - skip
            x_state["stores"] += 1
            for q in range(4):
                fin = None
                for n in range(8 * q, 8 * q + 8):
                    fin = DVE.emit("scalar_tensor_tensor", **dict(
                        out=x[:, n, :T], in0=x[:, n, :T], scalar=ppc("fg", n), in1=nrstd[:, :T], op0=ALU.mult, op1=ALU.mult),
                        deps=[rs_tk, x_tk[n]])
                nst["nrstd_free"] = fin
                tkq = ("s_o%d" % q, 16 * x_state["stores"], s_og[q])
                SP.emit("dma_start", **dict(
                    out=outT[8 * q:8 * q + 8, :, o0:o0 + To].rearrange("c p t -> p c t"), in_=x[:, 8 * q:8 * q + 8, skip:skip + To]),
                    deps=[fin], dma=tkq)
                x_state["store_tk"][q] = tkq

        assert W.consumed == len(plan), (W.consumed, len(plan))
        final_waits = list(x_state["store_tk"])

        with nc.Block() as block:
            @block.sync
            def _(e):
                SP.replay(e)
                for fw in final_waits:
                    e.wait_ge(fw[2], fw[1])

            @block.gpsimd
            def _(e):
                POOL.replay(e)

            @block.tensor
            def _(e):
                PE.replay(e)

            @block.scalar
            def _(e):
                ACT.replay(e)

            @block.vector
            def _(e):
                DVE.replay(e)
    return nc


def _prep_shared(norm_g, w_in, conv_w, conv_b, a_ln_g, a_ln_b, a_pw, b_ln_g, b_ln_b, b_ws, b_bias, c_w, c_scale, w_out, final_g):
    f = np.float32
    win = np.ascontiguousarray(np.asarray(w_in, f).reshape(L, 32, 128, 86, 128).transpose(0, 3, 2, 1, 4)).reshape(L, 86, 128, 4096)
    wpw = np.ascontiguousarray(np.asarray(a_pw, f).reshape(L, 12, 128, 12, 128).transpose(0, 3, 2, 1, 4)).reshape(L, 12, 128, 1536)
    wout = np.ascontiguousarray(np.asarray(w_out, f).reshape(L, 32, 128, 32, 128).transpose(0, 3, 2, 1, 4)).reshape(L, 32, 128, 4096)
    cwn = np.asarray(c_w, f)
    wbd = np.zeros((L, WC, WC), f)
    for g in range(4):
        wbd[:, g * 320:(g + 1) * 320, g * 320:(g + 1) * 320] = cwn[:, g]
    wcw = np.zeros((L, 10, 128, 640), f)
    for jo in range(10):
        for i, ji in enumerate(CW_JI[jo]):
            wcw[:, jo, :, i * 128:(i + 1) * 128] = wbd[:, ji * 128:(ji + 1) * 128, jo * 128:(jo + 1) * 128]
    wsr = np.ascontiguousarray(np.asarray(b_ws, f).transpose(0, 3, 1, 2)).reshape(L, 128, 1280)
    biasb = np.ascontiguousarray(np.broadcast_to(np.asarray(b_bias, f).reshape(L, 1, 1280), (L, 128, 1280)))
    pp = np.zeros((128, NPP), f)

    def put(name, arr):
        o = PP_OFF[name]
        pp[:, o:o + arr.shape[1]] = arr
    put("cw", np.asarray(conv_w, f).reshape(L, CONVW, 12, 128).transpose(3, 0, 2, 1).reshape(128, -1))
    for name, arr, nch in (("cb", conv_b, 12), ("alg", a_ln_g, 12), ("alb", a_ln_b, 12), ("blg", b_ln_g, 10),
                           ("blb", b_ln_b, 10), ("csc", c_scale, 10), ("ng", norm_g, 32)):
        put(name, np.asarray(arr, f).reshape(L, nch, 128).transpose(2, 0, 1).reshape(128, -1))
    put("fg", np.asarray(final_g, f).reshape(32, 128).T)
    ch = np.arange(WC)
    wnd = np.array(POOLW, f)[ch // 320]
    put("invw", (1.0 / wnd).astype(f).reshape(10, 128).T)
    return dict(win=win, wpw=wpw, wcw=wcw, wout=wout, wsr=wsr, biasb=biasb, pp=pp)


def _misc(seq_start):
    f = np.float32
    m = np.zeros((128, MISC_N), f)
    m[:, 0:128] = np.eye(128, dtype=f)
    q = np.arange(128)
    m[:, 128:256] = (q[:, None] <= q[None, :]).astype(f)
    ch = np.arange(WC)
    wnd = np.array(POOLW, f)[ch // 320]
    t = np.arange(16, dtype=f)
    if seq_start:
        corr = wnd[:, None] / np.minimum(t[None, :] + 1.0, wnd[:, None])
    else:
        corr = np.ones((WC, 16), f)
    m[:, 256:416] = corr.astype(f).reshape(10, 128, 16).transpose(1, 0, 2).reshape(128, 160)
    return m


_NC_CACHE = {}


def kernel(x, norm_g, w_in, conv_w, conv_b, a_ln_g, a_ln_b, a_pw, b_ln_g, b_ln_b, b_ws, b_bias, c_w, c_scale, w_out, final_g):
    x = np.asarray(x, np.float32)
    B, S, _ = x.shape
    shared = _prep_shared(norm_g, w_in, conv_w, conv_b, a_ln_g, a_ln_b, a_pw, b_ln_g, b_ln_b, b_ws, b_bias, c_w, c_scale, w_out, final_g)
    in_maps = []
    for c in range(NCORE):
        b, qd = divmod(c, NCORE // B)
        s0 = qd * TOK
        xt = np.zeros((LTOK, D), np.float32)
        if s0 > 0:
            xt[:] = x[b, s0 - HALO:s0 + TOK]
        else:
            xt[HALO:] = x[b, 0:TOK]
        m = dict(shared)
        m["xT"] = np.ascontiguousarray(xt.T).reshape(32, 128, LTOK)
        m["misc"] = _misc(s0 == 0)
        in_maps.append(m)
    if "nc" not in _NC_CACHE:
        _NC_CACHE["nc"] = build_nc()
    nc = _NC_CACHE["nc"]
    res = run_bass_kernel_spmd(nc, in_maps, core_ids=list(range(NCORE)))
    out = np.empty((B, S, D), np.float32)
    for c in range(NCORE):
        b, qd = divmod(c, NCORE // B)
        s0 = qd * TOK
        o = np.asarray(res.results[c]["outT"]).reshape(D, TOK)
        out[b, s0:s0 + TOK] = o.T
    return out
```

```python
import contextlib
import numpy as np
import concourse.bass as bass
import concourse.mybir as mybir
from concourse.bass_utils import run_bass_kernel_spmd

F32 = mybir.dt.float32
F32R = mybir.dt.float32r
BF16 = mybir.dt.bfloat16
AF = mybir.ActivationFunctionType
ALU = mybir.AluOpType

D = 4096
L = 2
WA, WB, WC = 1536, 1280, 1280
NCORE = 8
TOK = 2048
HALO = 128
LTOK = TOK + HALO
TMAX = 384
TILES = [(0, 384), (384, 384), (768, 384), (1152, 384), (1536, 384), (1920, 256)]
NW = 5
EPS = 1e-6
CONVW = 31
HH = 32
CH = 16
POOLW = (2, 4, 8, 16)

S_AVAL, S_AGLU, S_AGATE, S_BU, S_BV, S_BGATE, S_CIN, S_CGATE = 0, 12, 24, 36, 46, 56, 66, 76

def _chunk_groups(j):
    lo, hi = j * 128, j * 128 + 127
    return sorted(set([lo // 320, hi // 320]))
CW_JI = []
for _jo in range(10):
    gs = _chunk_groups(_jo)
    CW_JI.append([ji for ji in range(10) if set(_chunk_groups(ji)) & set(gs)])

def _pp_layout():
    off = {}
    o = 0
    for name, n in (("cw", L * 12 * CONVW), ("cb", L * 12), ("alg", L * 12), ("alb", L * 12),
                    ("blg", L * 10), ("blb", L * 10), ("csc", L * 10), ("ng", L * 32), ("fg", 32), ("invw", 10)):
        off[name] = o
        o += n
    return off, o
PP_OFF, NPP = _pp_layout()
PP_OFF["cwh"] = PP_OFF["cw"]
NX = 0
MISC_N = 128 + 128 + 160


class Stream:
    def __init__(self, name, sem, serial=False):
        self.name = name
        self.sem = sem
        self.serial = serial
        self.count = 0
        self.ops = []
        self.seen = {}

    def emit(self, _opname, deps=(), inc=True, dma=None, **kw):
        fn = (_opname, kw)
        waits = []
        deps = list(deps)
        if self.serial and self.count > 0 and dma is None:
            deps.append((self.name, self.count, self.sem))
        for d in deps:
            if d is None:
                continue
            key, val, sem = d
            if self.seen.get(key, 0) >= val:
                continue
            self.seen[key] = val
            waits.append((sem, val))
        tk = None
        if dma is not None:
            tk = dma
        elif inc:
            self.count += 1
            tk = (self.name, self.count, self.sem)
        self.ops.append((waits, fn, inc, dma))
        return tk

    def replay(self, eng):
        for waits, fn, inc, dma in self.ops:
            for sem, val in waits:
                eng.wait_ge(sem, val)
            ins = getattr(eng, fn[0])(**fn[1])
            if dma is not None:
                ins.then_inc(dma[2], 16)
            elif inc:
                ins.then_inc(self.sem, 1)


class Ring:
    def __init__(self, bufs):
        self.bufs = bufs
        self.i = 0
        self.rel = [[] for _ in bufs]

    def get(self):
        k = self.i % len(self.bufs)
        self.i += 1
        d = self.rel[k]
        self.rel[k] = []
        return k, self.bufs[k], d

    def release(self, k, *tickets):
        self.rel[k].extend(t for t in tickets if t is not None)


def build_nc(tiles=None, nlay=L, ltok=LTOK):
    tiles = TILES if tiles is None else tiles
    nc = bass.Bass("TRN2", target_bir_lowering=False)
    xT = nc.dram_tensor("xT", [32, 128, ltok], F32, kind="ExternalInput").ap()
    win = nc.dram_tensor("win", [L, 86, 128, 4096], F32, kind="ExternalInput").ap()
    wpw = nc.dram_tensor("wpw", [L, 12, 128, 1536], F32, kind="ExternalInput").ap()
    wcw = nc.dram_tensor("wcw", [L, 10, 128, 640], F32, kind="ExternalInput").ap()
    wout = nc.dram_tensor("wout", [L, 32, 128, 4096], F32, kind="ExternalInput").ap()
    wsr = nc.dram_tensor("wsr", [L, 128, 1280], F32, kind="ExternalInput").ap()
    ppd = nc.dram_tensor("pp", [128, NPP], F32, kind="ExternalInput").ap()
    biasd = nc.dram_tensor("biasb", [L, 128, 1280], F32, kind="ExternalInput").ap()
    miscd = nc.dram_tensor("misc", [128, MISC_N], F32, kind="ExternalInput").ap()
    outT = nc.dram_tensor("outT", [32, 128, ltok - HALO], F32, kind="ExternalOutput").ap()

    es = contextlib.ExitStack()
    with es:
        def sb(name, shape, dt):
            return es.enter_context(nc.sbuf_tensor(name, shape, dt))

        def ps(name, shape, dt=F32):
            return es.enter_context(nc.psum_tensor(name, shape, dt))

        def sem(name):
            return es.enter_context(nc.semaphore(name))

        x = sb("x", [128, 32, TMAX], F32)
        hT = sb("hT", [128, 32, TMAX], BF16)
        ycat = sb("ycat", [128, 32, TMAX], BF16)
        wb = [sb("w%d" % i, [128, 4096], BF16) for i in range(NW)]
        co = sb("co", [128, 12, TMAX], F32)
        hn = sb("hn", [128, 12, TMAX], BF16)
        hbuf = sb("hbuf", [128, HH + TMAX], F32)
        cbufs = [sb("cbuf%d" % i, [128, CH + TMAX], F32) for i in range(2)]
        ptA = sb("ptA", [128, CH + TMAX], F32)
        ptB = sb("ptB", [128, CH + TMAX], F32)
        stg = [sb("stg%d" % i, [128, TMAX], F32) for i in range(7)]
        rstd = sb("rstd", [128, TMAX], F32)
        nb = sb("nb", [128, TMAX], F32)
        nacc = sb("nacc", [128, TMAX], F32)
        mean = nacc
        nrstd = sb("nrstd", [128, TMAX], F32)
        vnb = [sb("vnb%d" % i, [128, TMAX], BF16) for i in range(2)]
        vnT = [sb("vnT%d" % i, [128, TMAX], BF16) for i in range(2)]
        htail = sb("htail", [128, L, 12, HH], F32)
        ctail = sb("ctail", [128, L, 10, CH], F32)
        wsT = sb("wsT", [128, L, 1280], BF16)
        bbuf = [sb("bb%d" % i, [128, 128], F32) for i in range(2)]
        pp = sb("pp_sb", [128, NPP + NX], F32)
        corr_sb = sb("corr_sb", [128, 160], F32)
        identb = sb("identb", [128, 128], BF16)
        onesA = sb("onesA", [128, 128], F32)
        onesB = sb("onesB", [128, 128], F32)
        onesN = sb("onesN", [128, 128], F32)

        print('sbuf bytes remaining', nc.sbuf_bytes_remaining)
        dC = ycat

        proj = [ps("pj%d" % i, [128, 512]) for i in range(3)]
        aux = [ps("ax%d" % i, [128, 512]) for i in range(2)]
        st0 = ps("st0", [128, 512])
        st1 = ps("st1", [128, 512])
        tps = ps("tps", [128, 1024], BF16)

        s_pe, s_act, s_dve, s_pool = sem("s_pe"), sem("s_act"), sem("s_dve"), sem("s_pool")
        s_w = [sem("s_w%d" % i) for i in range(NW)]
        s_p = sem("s_p")
        s_bb = [sem("s_bb%d" % i) for i in range(2)]
        bb_cnt = [0, 0]
        s_pre = [sem("s_pre%d" % i) for i in range(7)]
        pre_cnt = [0] * 7
        s_xg = [sem("s_x%d" % i) for i in range(4)]
        s_og = [sem("s_o%d" % i) for i in range(4)]

        PE = Stream("pe", s_pe)
        ACT = Stream("act", s_act, serial=True)
        DVE = Stream("dve", s_dve, serial=True)
        POOL = Stream("pool", s_pool, serial=True)
        SP = Stream("sp", None)

        PROJ = Ring(proj)
        AUX = Ring(aux)
        STG = Ring(stg)

        def ppc(name, idx):
            o = PP_OFF[name] + idx
            return pp[:, o:o + 1]

        plan = []

        def plan_cout(l, jo):
            plan.append((win[l, S_CGATE + jo], 4096))
            plan.append((wcw[l, jo, :, 0:len(CW_JI[jo]) * 128], len(CW_JI[jo]) * 128))

        for ti in range(len(tiles)):
            for l in range(nlay):
                for j in range(12):
                    if j >= 5:
                        plan_cout(l, j - 5)
                    plan.append((win[l, S_AGLU + j], 4096))
                    plan.append((win[l, S_AVAL + j], 4096))
                    if j < 10:
                        plan.append((win[l, S_CIN + j], 4096))
                for jo in (7, 8, 9):
                    plan_cout(l, jo)
                for j in range(10):
                    plan.append((win[l, S_BV + j], 4096))
                for j in range(12):
                    plan.append((win[l, S_AGATE + j], 4096))
                    plan.append((wpw[l, j], 1536))
                for j in range(10):
                    plan.append((win[l, S_BU + j], 4096))
                    plan.append((win[l, S_BGATE + j], 4096))
                for n in range(32):
                    plan.append((wout[l, n], 4096))

        class WMgr:
            def __init__(self):
                self.issued = 0
                self.consumed = 0
                self.free_tk = [None] * NW
                self.fill = [0] * NW
                self.load_tk = {}

            def issue_upto(self, n):
                while self.issued < min(n, len(plan)):
                    k = self.issued
                    b = k % NW
                    src, ncols = plan[k]
                    self.fill[b] += 1
                    tk = ("w%d" % b, 16 * self.fill[b], s_w[b])
                    dst = wb[b][:, 0:ncols]
                    POOL.emit("dma_start", **dict(out=dst, in_=src, max_dma_last_dim=8192),
                              deps=[self.free_tk[b]], dma=tk)
                    self.load_tk[k] = tk
                    self.issued += 1

            def take(self, ncols):
                k = self.consumed
                self.issue_upto(k + 1)
                assert plan[k][1] == ncols, (k, plan[k][1], ncols)
                return k, wb[k % NW], self.load_tk.pop(k)

            def done(self, k, tk):
                self.free_tk[k % NW] = tk
                self.consumed += 1
                self.issue_upto(k + NW + 1)

        W = WMgr()

        deferred = []
        slab_ctr = [0]

        def defer(delay, fn):
            deferred.append((slab_ctr[0] + delay, fn))

        def flush(force=False):
            while deferred and (force or deferred[0][0] <= slab_ctr[0]):
                _, fn = deferred.pop(0)
                fn()

        def slab_mm(out_ap, ncols, rhs_list, deps, deps_i=None):
            k, buf, ltk = W.take(ncols)
            n = len(rhs_list)
            assert n * 128 == ncols
            tk = None
            for i, rhs in enumerate(rhs_list):
                tk = PE.emit("matmul", **dict(out=out_ap, lhsT=buf[:, i * 128:(i + 1) * 128], rhs=rhs,
                                                                     start=(i == 0), stop=(i == n - 1)),
                             deps=(([ltk] + list(deps)) if i == 0 else []) + ([deps_i[i]] if deps_i else []), inc=(i == n - 1))
            W.done(k, tk)
            slab_ctr[0] += 1
            return tk

        tkp = ("s_p", 16 * 5, s_p)
        SP.emit("dma_start", **dict(out=pp[:, 0:NPP], in_=ppd), dma=("s_p", 16, s_p))
        SP.emit("dma_start", **dict(out=stg[0][:, 0:256], in_=miscd[:, 0:256]), dma=("s_p", 32, s_p))
        SP.emit("dma_start", **dict(out=corr_sb[:], in_=miscd[:, 256:416]), dma=("s_p", 48, s_p))
        costage = co[:, 0:8, 0:320]
        SP.emit("dma_start", **dict(out=co[:, 0:4, 0:320], in_=wsr[0].rearrange("q (a b) -> q a b", a=4)), dma=("s_p", 64, s_p))
        SP.emit("dma_start", **dict(out=co[:, 4:8, 0:320], in_=wsr[1].rearrange("q (a b) -> q a b", a=4)), dma=tkp)
        W.issue_upto(NW)
        tk0 = DVE.emit("tensor_copy", **dict(out=identb[:], in_=stg[0][:, 0:128]), deps=[tkp])
        DVE.emit("memset", **dict(ap=onesA[:], constant=1.0 / WA))
        DVE.emit("memset", **dict(ap=onesB[:], constant=1.0 / WB))
        DVE.emit("memset", **dict(ap=onesN[:], constant=1.0 / D))
        DVE.emit("memset", **dict(ap=htail[:], constant=0.0))
        tk_init = DVE.emit("memset", **dict(ap=ctail[:], constant=0.0))
        tk_half = DVE.emit("tensor_scalar", out=pp[:, PP_OFF["cw"]:PP_OFF["cw"] + L * 12 * CONVW],
                           in0=pp[:, PP_OFF["cw"]:PP_OFF["cw"] + L * 12 * CONVW], scalar1=0.5, scalar2=None, op0=ALU.mult, deps=[tkp])
        tk_ws = None
        for l in range(L):
            for h in range(10):
                a, r = divmod(h * 128, 320)
                for p0 in range(0, 128, 64):
                    col = h * 128 + p0
                    a, r = divmod(col, 320)
                    tk_ws = DVE.emit("tensor_tensor", **dict(
                        out=wsT[:, l, col:col + 64], in0=co[:, 4 * l + a, r:r + 64], in1=stg[0][:, 128 + p0:128 + p0 + 64], op=ALU.mult),
                        deps=[tkp])
        STG.release(0, tk_ws)
        co_free = [tk_ws]
        hT_free = [None]
        ycat_free = [None]
        stat_free = [[], []]
        htail_tk = [[tk_init] * 12 for _ in range(L)]
        ctail_tk = [[tk_init] * 10 for _ in range(L)]
        x_state = {"loads": 0, "stores": 0, "store_tk": [None] * 4}

        def norm_accum(n, T, xtk, st):
            k, sbuf_, sdeps = STG.get()
            tsq = ACT.emit("activation", **dict(out=sbuf_[:, :T], in_=x[:, n, :T], func=AF.Square), deps=[xtk] + sdeps)
            if n == 0:
                t = DVE.emit("tensor_copy", **dict(out=nacc[:, :T], in_=sbuf_[:, :T]), deps=[tsq] + st["nacc_free"])
            else:
                t = DVE.emit("tensor_tensor", **dict(out=nacc[:, :T], in0=nacc[:, :T], in1=sbuf_[:, :T], op=ALU.add),
                             deps=[tsq, st["nacc_tk"]])
            st["nacc_tk"] = t
            STG.release(k, t)

        def norm_finalize(T, st, acc=None, acc_tk=None):
            if acc is None:
                acc, acc_tk = nacc, st["nacc_tk"]
            tpe = PE.emit("matmul", **dict(out=st0[:, :T], lhsT=onesN[:], rhs=acc[:, :T], start=True, stop=True),
                          deps=[acc_tk] + stat_free[0])
            stat_free[0] = []
            if acc is nacc:
                st["nacc_free"] = [tpe]
            ta = ACT.emit("activation", out=nrstd[:, :T], in_=st0[:, :T], func=AF.Sqrt, bias=EPS, scale=1.0,
                          deps=[tpe, st.get("nrstd_free")])
            stat_free[0].append(ta)
            td = DVE.emit("reciprocal", out=nrstd[:, :T], in_=nrstd[:, :T], deps=[ta])
            return td

        nst = {"nacc_free": [], "nacc_tk": None}
        pers = {}

        for ti, (t0, T) in enumerate(tiles):
            C = T // 128
            x_state["loads"] += 1
            xg_tk = []
            for q in range(4):
                tkq = ("s_x%d" % q, 16 * x_state["loads"], s_xg[q])
                SP.emit("dma_start", **dict(out=x[:, 8 * q:8 * q + 8, 0:T],
                                            in_=xT[8 * q:8 * q + 8, :, t0:t0 + T].rearrange("c p t -> p c t")),
                        deps=[x_state["store_tk"][q]], dma=tkq)
                xg_tk.append(tkq)
            if pers.get("pre_tk") is not None:
                rs_tk = norm_finalize(T, nst, acc=rstd, acc_tk=pers["pre_tk"])
                pers["pre_tk"] = None
            else:
                for n in range(32):
                    norm_accum(n, T, xg_tk[n // 8], nst)
                rs_tk = norm_finalize(T, nst)
            x_tk = [xg_tk[n // 8] for n in range(32)]

            for l in range(nlay):
                btks = []
                for j in range(10):
                    bb_cnt[j % 2] += 1
                    btk_j = ("s_bb%d" % (j % 2), 16 * bb_cnt[j % 2], s_bb[j % 2])
                    btks.append(btk_j)

                ht_tk = None
                ht_tks = []
                for n in range(32):
                    ht_tk = DVE.emit("scalar_tensor_tensor", **dict(
                        out=hT[:, n, :T], in0=x[:, n, :T], scalar=ppc("ng", l * 32 + n), in1=nrstd[:, :T],
                        op0=ALU.mult, op1=ALU.mult), deps=[rs_tk, x_tk[n], hT_free[0], tkp])
                    ht_tks.append(ht_tk)
                nst["nrstd_free"] = ht_tk
                hT_rhs = [hT[:, kc, :T] for kc in range(32)]

                conv_tk = [None] * 12
                dC_tk = [None] * 10
                lnA = {}

                def ln_stats(j, nch, ones, src_tk, holder, T=T):
                    k, sbuf_, sdeps = STG.get()
                    tsq = ACT.emit("activation", out=sbuf_[:, :T], in_=co[:, j, :T], func=AF.Square, deps=[src_tk] + sdeps)
                    if j == 0:
                        a1 = POOL.emit("tensor_copy", out=nacc[:, :T], in_=co[:, j, :T],
                                       deps=[src_tk, pers.get("mean_free")] + nst["nacc_free"])
                        a2 = POOL.emit("tensor_copy", out=nb[:, :T], in_=sbuf_[:, :T], deps=[tsq, pers.get("nb_free")])
                    else:
                        a1 = POOL.emit("tensor_tensor", out=nacc[:, :T], in0=nacc[:, :T], in1=co[:, j, :T], op=ALU.add, deps=[src_tk])
                        a2 = POOL.emit("tensor_tensor", out=nb[:, :T], in0=nb[:, :T], in1=sbuf_[:, :T], op=ALU.add, deps=[tsq])
                    STG.release(k, a2)
                    if j == nch - 1:
                        d0, d1 = stat_free[0], stat_free[1]
                        stat_free[0] = []
                        stat_free[1] = []
                        PE.emit("matmul", out=st0[:, :T], lhsT=ones[:], rhs=nacc[:, :T], start=True, stop=True, deps=[a1] + d0, inc=False)
                        te = PE.emit("matmul", out=st1[:, :T], lhsT=ones[:], rhs=nb[:, :T], start=True, stop=True, deps=[a2] + d1)
                        holder["pe"] = te
                        nst["nacc_free"] = [te]

                def lnA_stats(j):
                    ln_stats(j, 12, onesA, conv_tk[j], lnA)

                def ln_finalize(pe_tk, T):
                    t1 = DVE.emit("tensor_copy", **dict(out=mean[:, :T], in_=st0[:, :T]), deps=[pe_tk] + nst["nacc_free"])
                    t2 = DVE.emit("tensor_tensor", out=rstd[:, :T], in0=mean[:, :T], in1=mean[:, :T], op=ALU.mult, deps=[t1])
                    t3 = DVE.emit("tensor_tensor", out=rstd[:, :T], in0=st1[:, :T], in1=rstd[:, :T], op=ALU.subtract, deps=[t2])
                    t3b = DVE.emit("tensor_scalar", out=rstd[:, :T], in0=rstd[:, :T], scalar1=0.0, scalar2=None, op0=ALU.max, deps=[t3])
                    ta = ACT.emit("activation", out=rstd[:, :T], in_=rstd[:, :T], func=AF.Sqrt, bias=EPS, scale=1.0, deps=[t3b])
                    t4 = DVE.emit("reciprocal", **dict(out=rstd[:, :T], in_=rstd[:, :T]), deps=[ta])
                    t5 = DVE.emit("scalar_tensor_tensor", **dict(out=nb[:, :T], in0=mean[:, :T], scalar=-1.0, in1=rstd[:, :T],
                                                                   op0=ALU.mult, op1=ALU.mult), deps=[t4])
                    stat_free[0].append(t1)
                    stat_free[1].append(t3)
                    pers["mean_free"] = t5
                    return t5

                def lnA_finish(T=T, l=l):
                    t5 = ln_finalize(lnA["pe"], T)
                    last = None
                    for j in range(12):
                        a = DVE.emit("tensor_tensor", **dict(out=co[:, j, :T], in0=co[:, j, :T], in1=rstd[:, :T], op=ALU.mult),
                                     deps=[t5, lnA["pe"]])
                        b = DVE.emit("tensor_tensor", **dict(out=co[:, j, :T], in0=co[:, j, :T], in1=nb[:, :T], op=ALU.add),
                                     deps=[a])
                        pers["nb_free"] = b
                        last = lnA.setdefault("hn_tks", [None] * 12)[j] = ACT.emit("activation", **dict(out=hn[:, j, :T], in_=co[:, j, :T], func=AF.Silu,
                                                                    bias=ppc("alb", l * 12 + j), scale=ppc("alg", l * 12 + j)),
                                        deps=[b, pers.get("hn_free")])
                    lnA["hn_tk"] = last

                cw_state = {"last": None}
                yc_tk = [None] * 32

                def c_out(jo, T=T, l=l):
                    pk, pbuf, pdeps = PROJ.get()
                    tpe = slab_mm(pbuf[:, :T], 4096, hT_rhs, pdeps)
                    sk, sbuf_, sdeps = STG.get()
                    tsg = ACT.emit("activation", out=sbuf_[:, :T], in_=pbuf[:, :T], func=AF.Silu, deps=[tpe] + sdeps)
                    PROJ.release(pk, tsg)
                    ak, abuf, adeps = AUX.get()
                    jis = CW_JI[jo]
                    tcw = slab_mm(abuf[:, :T], len(jis) * 128, [dC[:, 12 + ji, :T] for ji in jis],
                                  [dC_tk[ji] for ji in jis] + adeps)
                    cw_state["last"] = tcw
                    sk2, tmp_, sdeps2 = STG.get()
                    ttmp = ACT.emit("activation", out=tmp_[:, :T], in_=abuf[:, :T], func=AF.Identity, scale=ppc("csc", l * 10 + jo),
                                    deps=[tcw, tkp] + sdeps2)
                    AUX.release(ak, ttmp)
                    ty = POOL.emit("tensor_tensor", out=ycat[:, 22 + jo, :T], in0=tmp_[:, :T], in1=sbuf_[:, :T], op=ALU.mult,
                                   deps=[ttmp, tsg, ycat_free[0]])
                    yc_tk[22 + jo] = ty
                    STG.release(sk, ty)
                    STG.release(sk2, ty)
                    flush()

                for j in range(12):
                    if j >= 5:
                        c_out(j - 5)
                    pk, pbuf, pdeps = PROJ.get()
                    tpe = slab_mm(pbuf[:, :T], 4096, hT_rhs, pdeps, deps_i=(ht_tks if j == 0 else None))
                    sk, sbuf_, sdeps = STG.get()
                    tsig = ACT.emit("activation", out=sbuf_[:, :T], in_=pbuf[:, :T], func=AF.Tanh, scale=0.5, deps=[tpe] + sdeps)
                    PROJ.release(pk, tsig)
                    flush()
                    pk2, pbuf2, pdeps2 = PROJ.get()
                    tpe2 = slab_mm(pbuf2[:, :T], 4096, hT_rhs, pdeps2)
                    skv, vbuf_, sdepsv = STG.get()
                    tval = ACT.emit("activation", out=vbuf_[:, :T], in_=pbuf2[:, :T], func=AF.Copy, deps=[tpe2] + sdepsv)
                    PROJ.release(pk2, tval)
                    th0 = DVE.emit("tensor_copy", out=hbuf[:, 0:HH], in_=htail[:, l, j, :], deps=[htail_tk[l][j]])
                    tglu = DVE.emit("scalar_tensor_tensor", out=hbuf[:, HH:HH + T], in0=sbuf_[:, :T], scalar=1.0, in1=vbuf_[:, :T],
                                    op0=ALU.add, op1=ALU.mult, deps=[tval, tsig])
                    STG.release(sk, tglu)
                    STG.release(skv, tglu)
                    htail_tk[l][j] = DVE.emit("tensor_copy", out=htail[:, l, j, :], in_=hbuf[:, T:T + HH], deps=[tglu])
                    base = (l * 12 + j) * CONVW
                    tk = DVE.emit("tensor_scalar", out=co[:, j, :T], in0=hbuf[:, 2:2 + T], scalar1=ppc("cwh", base),
                                  scalar2=ppc("cb", l * 12 + j), op0=ALU.mult, op1=ALU.add, deps=[th0, tglu, tk_half] + co_free)
                    co_free = []
                    for k in range(1, CONVW):
                        tk = DVE.emit("scalar_tensor_tensor", out=co[:, j, :T], in0=hbuf[:, 2 + k:2 + k + T], scalar=ppc("cwh", base + k),
                                      in1=co[:, j, :T], op0=ALU.mult, op1=ALU.add, deps=[tk])
                    conv_tk[j] = tk
                    defer(9, lambda j=j: lnA_stats(j))
                    flush()
                    if j < 10:
                        pk3, pbuf3, pdeps3 = PROJ.get()
                        tpe3 = slab_mm(pbuf3[:, :T], 4096, hT_rhs, pdeps3)
                        cbuf = cbufs[j % 2]
                        tc1 = ACT.emit("activation", out=cbuf[:, CH:CH + T], in_=pbuf3[:, :T], func=AF.Copy,
                                       deps=[tpe3, pers.get("cbuf_free%d" % (j % 2))])
                        PROJ.release(pk3, tc1)
                        tc0 = POOL.emit("tensor_copy", out=cbuf[:, 0:CH], in_=ctail[:, l, j, :],
                                        deps=[ctail_tk[l][j], pers.get("cbuf_free%d" % (j % 2))])
                        ctail_tk[l][j] = POOL.emit("tensor_copy", out=ctail[:, l, j, :], in_=cbuf[:, T:T + CH], deps=[tc1])
                        gs = _chunk_groups(j)
                        parts = [(0, 128, gs[0])] if len(gs) == 1 else [(0, 64, gs[0]), (64, 128, gs[1])]
                        E = CH + T
                        srcs = [cbuf, ptA, ptB, ptA, ptB]
                        prev = [tc0, tc1]
                        final_src = {}
                        for step, sh in enumerate((1, 2, 4, 8)):
                            wnd = 2 * sh
                            act_parts = [(a_, b_) for (a_, b_, g) in parts if POOLW[g] >= wnd]
                            if not act_parts:
                                break
                            lo, hi = min(a_ for a_, b_ in act_parts), max(b_ for a_, b_ in act_parts)
                            s_in, s_out = srcs[step], srcs[step + 1]
                            st_ = wnd - 1
                            tkp_ = POOL.emit("tensor_tensor", out=s_out[lo:hi, st_:E], in0=s_in[lo:hi, st_:E],
                                             in1=s_in[lo:hi, st_ - sh:E - sh], op=ALU.add, deps=prev + [pers.get("pt_free")])
                            prev = [tkp_]
                            for (a_, b_, g) in parts:
                                if POOLW[g] == wnd:
                                    final_src[(a_, b_)] = s_out
                        tlast = None
                        for (a_, b_, g) in parts:
                            fs = final_src[(a_, b_)]
                            if ti == 0:
                                c0 = CH + HALO
                                tfix = DVE.emit("tensor_tensor", out=fs[a_:b_, c0:c0 + 16], in0=fs[a_:b_, c0:c0 + 16],
                                                in1=corr_sb[a_:b_, j * 16:j * 16 + 16], op=ALU.mult, deps=prev + [tkp])
                                prev = [tfix]
                            tlast = DVE.emit("scalar_tensor_tensor", out=dC[a_:b_, 12 + j, :T], in0=fs[a_:b_, CH:CH + T],
                                             scalar=pp[a_:b_, PP_OFF["invw"] + j:PP_OFF["invw"] + j + 1],
                                             in1=cbuf[a_:b_, CH:CH + T], op0=ALU.mult, op1=ALU.subtract, deps=prev + [ycat_free[0]])
                            prev = [tlast]
                        dC_tk[j] = tlast
                        pers["cbuf_free%d" % (j % 2)] = tlast
                        pers["pt_free"] = tlast
                        flush()
                defer(10, lnA_finish)
                for jo in (7, 8, 9):
                    c_out(jo)
                flush(force=True)
                cw_last = cw_state["last"]

                v_tk = [None] * 10
                lnB = {}

                def lnB_stats(j):
                    ln_stats(j, 10, onesB, v_tk[j], lnB)

                for j in range(10):
                    pk, pbuf, pdeps = PROJ.get()
                    tpe = slab_mm(pbuf[:, :T], 4096, hT_rhs, pdeps)
                    v_tk[j] = ACT.emit("activation", out=co[:, j, :T], in_=pbuf[:, :T], func=AF.Gelu_apprx_tanh,
                                       deps=[tpe, lnA["hn_tks"][j]])
                    PROJ.release(pk, v_tk[j])
                    defer(1, lambda j=j: lnB_stats(j))
                    flush()

                def vn_make(j, t5, T=T, l=l):
                    a = DVE.emit("tensor_tensor", out=co[:, j, :T], in0=co[:, j, :T], in1=rstd[:, :T], op=ALU.mult,
                                 deps=[t5, lnB["pe"]])
                    b = DVE.emit("tensor_tensor", out=co[:, j, :T], in0=co[:, j, :T], in1=nb[:, :T], op=ALU.add, deps=[a])
                    pers["nb_free"] = b
                    vb = vnb[j % 2]
                    c = ACT.emit("activation", out=vb[:, :T], in_=co[:, j, :T], func=AF.Identity,
                                 bias=ppc("blb", l * 10 + j), scale=ppc("blg", l * 10 + j),
                                 deps=[b, pers.get("vnb_free%d" % (j % 2))])
                    return c

                pw_last = None
                t5B = None
                vn_tk = [None] * 10
                for j in range(12):
                    pk, pbuf, pdeps = PROJ.get()
                    tpe = slab_mm(pbuf[:, :T], 4096, hT_rhs, pdeps)
                    flush()
                    if j == 0:
                        flush(force=True)
                        t5B = ln_finalize(lnB["pe"], T)
                        vn_tk[0] = vn_make(0, t5B)
                        vn_tk[1] = vn_make(1, t5B)
                    sk, sbuf_, sdeps = STG.get()
                    tsg = ACT.emit("activation", out=sbuf_[:, :T], in_=pbuf[:, :T], func=AF.Silu, deps=[tpe] + sdeps)
                    PROJ.release(pk, tsg)
                    ak, abuf, adeps = AUX.get()
                    tpw = slab_mm(abuf[:, :T], 1536, [hn[:, kc, :T] for kc in range(12)], [lnA["hn_tk"]] + adeps)
                    pw_last = tpw
                    ty = DVE.emit("tensor_tensor", out=ycat[:, j, :T], in0=abuf[:, :T], in1=sbuf_[:, :T], op=ALU.mult,
                                  deps=[tpw, tsg, ycat_free[0]])
                    yc_tk[j] = ty
                    AUX.release(ak, ty)
                    STG.release(sk, ty)
                    flush()
                pers["hn_free"] = pw_last

                hb = {}
                sp_state = {"last": None}

                def spatial_and_out(j, T=T, l=l, C=C):
                    sk, gu, tm, vt, tev = hb.pop(j)
                    ak, abuf, adeps = AUX.get()
                    tsp = None
                    for c in range(C):
                        tsp = PE.emit("matmul", out=abuf[:, c * 128:(c + 1) * 128], lhsT=vt[:, c * 128:(c + 1) * 128],
                                      rhs=wsT[:, l, j * 128:(j + 1) * 128], start=True, stop=True,
                                      deps=([tev, tk_ws] + adeps) if c == 0 else (), inc=(c == C - 1))
                    pers["vnT_free%d" % (j % 2)] = tsp
                    sp_state["last"] = tsp
                    sk3, tb, sdeps3 = STG.get()
                    bb = bbuf[j % 2]
                    SP.emit("dma_start", out=bb[:], in_=biasd[l, :, j * 128:(j + 1) * 128],
                            deps=[pers.get("bb_free%d" % (j % 2))], dma=btks[j])
                    tadd = None
                    for c in range(C):
                        tadd = DVE.emit("tensor_tensor", out=tb[:, c * 128:(c + 1) * 128], in0=abuf[:, c * 128:(c + 1) * 128],
                                        in1=bb[:], op=ALU.add,
                                        deps=([tsp, btks[j]] + sdeps3) if c == 0 else ())
                    pers["bb_free%d" % (j % 2)] = tadd
                    AUX.release(ak, tadd)
                    ty = DVE.emit("scalar_tensor_tensor", out=ycat[:, 12 + j, :T], in0=tb[:, :T], scalar=0.5, in1=gu[:, :T],
                                  op0=ALU.mult, op1=ALU.mult, deps=[tadd, tm, cw_last])
                    yc_tk[12 + j] = ty
                    STG.release(sk, ty)
                    STG.release(sk3, ty)

                for j in range(10):
                    pk, pbuf, pdeps = PROJ.get()
                    tpe = slab_mm(pbuf[:, :T], 4096, hT_rhs, pdeps)
                    sk, gu, sdeps = STG.get()
                    tgu = ACT.emit("activation", out=gu[:, :T], in_=pbuf[:, :T], func=AF.Gelu_apprx_tanh, deps=[tpe] + sdeps)
                    PROJ.release(pk, tgu)
                    if j >= 1:
                        spatial_and_out(j - 1)
                    pk2, pbuf2, pdeps2 = PROJ.get()
                    tpe2 = slab_mm(pbuf2[:, :T], 4096, hT_rhs, pdeps2)
                    sk2, th, sdeps2 = STG.get()
                    tth = ACT.emit("activation", out=th[:, :T], in_=pbuf2[:, :T], func=AF.Tanh, scale=0.5, deps=[tpe2] + sdeps2)
                    tsg = DVE.emit("scalar_tensor_tensor", out=th[:, :T], in0=th[:, :T], scalar=1.0, in1=pbuf2[:, :T],
                                   op0=ALU.add, op1=ALU.mult, deps=[tth])
                    PROJ.release(pk2, tsg)
                    tm = DVE.emit("tensor_tensor", out=gu[:, :T], in0=gu[:, :T], in1=th[:, :T], op=ALU.mult, deps=[tsg, tgu])
                    STG.release(sk2, tm)
                    vb = vnb[j % 2]
                    ttp = None
                    for c in range(C):
                        ttp = PE.emit("transpose", out=tps[:, c * 128:(c + 1) * 128], in_=vb[:, c * 128:(c + 1) * 128], identity=identb[:],
                                      deps=[vn_tk[j], tk0, pers.get("tps_free")] if c == 0 else (), inc=(c == C - 1))
                    pers["vnb_free%d" % (j % 2)] = ttp
                    vt = vnT[j % 2]
                    tev = DVE.emit("tensor_copy", out=vt[:, :T], in_=tps[:, :T], deps=[ttp, pers.get("vnT_free%d" % (j % 2))])
                    pers["tps_free"] = tev
                    hb[j] = (sk, gu, tm, vt, tev)
                    if j + 2 < 10:
                        vn_tk[j + 2] = vn_make(j + 2, t5B)
                    flush()
                spatial_and_out(9)
                hT_free[0] = ("pe", PE.count, s_pe)
                co_free = [sp_state["last"], lnB["pe"]]

                ycat_rhs = [ycat[:, kc, :T] for kc in range(32)]
                wo_last = None
                pend = None
                for n in range(32):
                    pk, pbuf, pdeps = PROJ.get()
                    tpe = slab_mm(pbuf[:, :T], 4096, ycat_rhs, (yc_tk if n == 0 else []) + pdeps)
                    wo_last = tpe
                    tx = DVE.emit("tensor_tensor", **dict(out=x[:, n, :T], in0=pbuf[:, :T], in1=x[:, n, :T], op=ALU.add),
                                  deps=[tpe, ht_tk])
                    PROJ.release(pk, tx)
                    x_tk[n] = tx
                    norm_accum(n, T, tx, nst)
                    if l == nlay - 1 and ti + 1 < len(tiles):
                        nt0, nT = tiles[ti + 1]
                        k, sbuf_, sdeps = STG.get()
                        pre_cnt[k] += 1
                        tld = ("s_pre%d" % k, 16 * pre_cnt[k], s_pre[k])
                        SP.emit("dma_start", out=sbuf_[:, :nT], in_=xT[n, :, nt0:nt0 + nT], deps=sdeps, dma=tld)
                        tsq = ACT.emit("activation", out=sbuf_[:, :nT], in_=sbuf_[:, :nT], func=AF.Square, deps=[tld])
                        if n == 0:
                            tacc = DVE.emit("tensor_copy", out=rstd[:, :nT], in_=sbuf_[:, :nT], deps=[tsq])
                        else:
                            tacc = DVE.emit("tensor_tensor", out=rstd[:, :nT], in0=rstd[:, :nT], in1=sbuf_[:, :nT], op=ALU.add, deps=[tsq])
                        STG.release(k, tacc)
                        pers["pre_tk"] = tacc
                    flush()
                ycat_free[0] = wo_last
                rs_tk = norm_finalize(T, nst)

            skip = HALO if ti == 0 else 0
            o0 = t0 + skip - HALO
            To = T - skip
            x_state["stores"] += 1
            for q in range(4):
                fin = None
                for n in range(8 * q, 8 * q + 8):
                    fin = DVE.emit("scalar_tensor_tensor", **dict(
                        out=x[:, n, :T], in0=x[:, n, :T], scalar=ppc("fg", n), in1=nrstd[:, :T], op0=ALU.mult, op1=ALU.mult),
                        deps=[rs_tk, x_tk[n]])
                nst["nrstd_free"] = fin
                tkq = ("s_o%d" % q, 16 * x_state["stores"], s_og[q])
                SP.emit("dma_start", **dict(
                    out=outT[8 * q:8 * q + 8, :, o0:o0 + To].rearrange("c p t -> p c t"), in_=x[:, 8 * q:8 * q + 8, skip:skip + To]),
                    deps=[fin], dma=tkq)
                x_state["store_tk"][q] = tkq

        assert W.consumed == len(plan), (W.consumed, len(plan))
        final_waits = list(x_state["store_tk"])

        with nc.Block() as block:
            @block.sync
            def _(e):
                SP.replay(e)
                for fw in final_waits:
                    e.wait_ge(fw[2], fw[1])

            @block.gpsimd
            def _(e):
                POOL.replay(e)

            @block.tensor
            def _(e):
                PE.replay(e)

            @block.scalar
            def _(e):
                ACT.replay(e)

            @block.vector
            def _(e):
                DVE.replay(e)
    return nc


def _prep_shared(norm_g, w_in, conv_w, conv_b, a_ln_g, a_ln_b, a_pw, b_ln_g, b_ln_b, b_ws, b_bias, c_w, c_scale, w_out, final_g):
    f = np.float32
    win = np.ascontiguousarray(np.asarray(w_in, f).reshape(L, 32, 128, 86, 128).transpose(0, 3, 2, 1, 4)).reshape(L, 86, 128, 4096)
    wpw = np.ascontiguousarray(np.asarray(a_pw, f).reshape(L, 12, 128, 12, 128).transpose(0, 3, 2, 1, 4)).reshape(L, 12, 128, 1536)
    wout = np.ascontiguousarray(np.asarray(w_out, f).reshape(L, 32, 128, 32, 128).transpose(0, 3, 2, 1, 4)).reshape(L, 32, 128, 4096)
    cwn = np.asarray(c_w, f)
    wbd = np.zeros((L, WC, WC), f)
    for g in range(4):
        wbd[:, g * 320:(g + 1) * 320, g * 320:(g + 1) * 320] = cwn[:, g]
    wcw = np.zeros((L, 10, 128, 640), f)
    for jo in range(10):
        for i, ji in enumerate(CW_JI[jo]):
            wcw[:, jo, :, i * 128:(i + 1) * 128] = wbd[:, ji * 128:(ji + 1) * 128, jo * 128:(jo + 1) * 128]
    wsr = np.ascontiguousarray(np.asarray(b_ws, f).transpose(0, 3, 1, 2)).reshape(L, 128, 1280)
    biasb = np.ascontiguousarray(np.broadcast_to(np.asarray(b_bias, f).reshape(L, 1, 1280), (L, 128, 1280)))
    pp = np.zeros((128, NPP), f)

    def put(name, arr):
        o = PP_OFF[name]
        pp[:, o:o + arr.shape[1]] = arr
    put("cw", np.asarray(conv_w, f).reshape(L, CONVW, 12, 128).transpose(3, 0, 2, 1).reshape(128, -1))
    for name, arr, nch in (("cb", conv_b, 12), ("alg", a_ln_g, 12), ("alb", a_ln_b, 12), ("blg", b_ln_g, 10),
                           ("blb", b_ln_b, 10), ("csc", c_scale, 10), ("ng", norm_g, 32)):
        put(name, np.asarray(arr, f).reshape(L, nch, 128).transpose(2, 0, 1).reshape(128, -1))
    put("fg", np.asarray(final_g, f).reshape(32, 128).T)
    ch = np.arange(WC)
    wnd = np.array(POOLW, f)[ch // 320]
    put("invw", (1.0 / wnd).astype(f).reshape(10, 128).T)
    return dict(win=win, wpw=wpw, wcw=wcw, wout=wout, wsr=wsr, biasb=biasb, pp=pp)


def _misc(seq_start):
    f = np.float32
    m = np.zeros((128, MISC_N), f)
    m[:, 0:128] = np.eye(128, dtype=f)
    q = np.arange(128)
    m[:, 128:256] = (q[:, None] <= q[None, :]).astype(f)
    ch = np.arange(WC)
    wnd = np.array(POOLW, f)[ch // 320]
    t = np.arange(16, dtype=f)
    if seq_start:
        corr = wnd[:, None] / np.minimum(t[None, :] + 1.0, wnd[:, None])
    else:
        corr = np.ones((WC, 16), f)
    m[:, 256:416] = corr.astype(f).reshape(10, 128, 16).transpose(1, 0, 2).reshape(128, 160)
    return m


_NC_CACHE = {}


def kernel(x, norm_g, w_in, conv_w, conv_b, a_ln_g, a_ln_b, a_pw, b_ln_g, b_ln_b, b_ws, b_bias, c_w, c_scale, w_out, final_g):
    x = np.asarray(x, np.float32)
    B, S, _ = x.shape
    shared = _prep_shared(norm_g, w_in, conv_w, conv_b, a_ln_g, a_ln_b, a_pw, b_ln_g, b_ln_b, b_ws, b_bias, c_w, c_scale, w_out, final_g)
    in_maps = []
    for c in range(NCORE):
        b, qd = divmod(c, NCORE // B)
        s0 = qd * TOK
        xt = np.zeros((LTOK, D), np.float32)
        if s0 > 0:
            xt[:] = x[b, s0 - HALO:s0 + TOK]
        else:
            xt[HALO:] = x[b, 0:TOK]
        m = dict(shared)
        m["xT"] = np.ascontiguousarray(xt.T).reshape(32, 128, LTOK)
        m["misc"] = _misc(s0 == 0)
        in_maps.append(m)
    if "nc" not in _NC_CACHE:
        _NC_CACHE["nc"] = build_nc()
    nc = _NC_CACHE["nc"]
    res = run_bass_kernel_spmd(nc, in_maps, core_ids=list(range(NCORE)))
    out = np.empty((B, S, D), np.float32)
    for c in range(NCORE):
        b, qd = divmod(c, NCORE // B)
        s0 = qd * TOK
        o = np.asarray(res.results[c]["outT"]).reshape(D, TOK)
        out[b, s0:s0 + TOK] = o.T
    return out
```

```python
import contextlib
import numpy as np
import concourse.bass as bass
import concourse.mybir as mybir
from concourse.bass_utils import run_bass_kernel_spmd

F32 = mybir.dt.float32
F32R = mybir.dt.float32r
BF16 = mybir.dt.bfloat16
AF = mybir.ActivationFunctionType
ALU = mybir.AluOpType

D = 4096
L = 2
WA, WB, WC = 1536, 1280, 1280
NCORE = 8
TOK = 2048
HALO = 128
LTOK = TOK + HALO
TMAX = 384
TILES = [(0, 384), (384, 384), (768, 384), (1152, 384), (1536, 384), (1920, 256)]
NW = 5
EPS = 1e-6
CONVW = 31
HH = 32
CH = 16
POOLW = (2, 4, 8, 16)

S_AVAL, S_AGLU, S_AGATE, S_BU, S_BV, S_BGATE, S_CIN, S_CGATE = 0, 12, 24, 36, 46, 56, 66, 76

def _chunk_groups(j):
    lo, hi = j * 128, j * 128 + 127
    return sorted(set([lo // 320, hi // 320]))
CW_JI = []
for _jo in range(10):
    gs = _chunk_groups(_jo)
    CW_JI.append([ji for ji in range(10) if set(_chunk_groups(ji)) & set(gs)])

def _pp_layout():
    off = {}
    o = 0
    for name, n in (("cw", L * 12 * CONVW), ("cb", L * 12), ("alg", L * 12), ("alb", L * 12),
                    ("blg", L * 10), ("blb", L * 10), ("csc", L * 10), ("ng", L * 32), ("fg", 32), ("invw", 10)):
        off[name] = o
        o += n
    return off, o
PP_OFF, NPP = _pp_layout()
PP_OFF["cwh"] = PP_OFF["cw"]
NX = 0
MISC_N = 128 + 128 + 160


class Stream:
    def __init__(self, name, sem, serial=False):
        self.name = name
        self.sem = sem
        self.serial = serial
        self.count = 0
        self.ops = []
        self.seen = {}

    def emit(self, _opname, deps=(), inc=True, dma=None, noserial=False, **kw):
        fn = (_opname, kw)
        waits = []
        deps = list(deps)
        if self.serial and self.count > 0 and dma is None and not noserial:
            deps.append((self.name, self.count, self.sem))
        for d in deps:
            if d is None:
                continue
            key, val, sem = d
            if self.seen.get(key, 0) >= val:
                continue
            self.seen[key] = val
            waits.append((sem, val))
        tk = None
        if dma is not None:
            tk = dma
        elif inc:
            self.count += 1
            tk = (self.name, self.count, self.sem)
        self.ops.append((waits, fn, inc, dma))
        return tk

    def replay(self, eng):
        for waits, fn, inc, dma in self.ops:
            for sem, val in waits:
                eng.wait_ge(sem, val)
            ins = getattr(eng, fn[0])(**fn[1])
            if dma is not None:
                ins.then_inc(dma[2], 16)
            elif inc:
                ins.then_inc(self.sem, 1)


class Ring:
    def __init__(self, bufs):
        self.bufs = bufs
        self.i = 0
        self.rel = [[] for _ in bufs]

    def get(self):
        k = self.i % len(self.bufs)
        self.i += 1
        d = self.rel[k]
        self.rel[k] = []
        return k, self.bufs[k], d

    def release(self, k, *tickets):
        self.rel[k].extend(t for t in tickets if t is not None)


def build_nc(tiles=None, nlay=L, ltok=LTOK):
    tiles = TILES if tiles is None else tiles
    nc = bass.Bass("TRN2", target_bir_lowering=False)
    xT = nc.dram_tensor("xT", [32, 128, ltok], F32, kind="ExternalInput").ap()
    win = nc.dram_tensor("win", [L, 86, 128, 4096], F32, kind="ExternalInput").ap()
    wpw = nc.dram_tensor("wpw", [L, 12, 128, 1536], F32, kind="ExternalInput").ap()
    wcw = nc.dram_tensor("wcw", [L, 10, 128, 640], F32, kind="ExternalInput").ap()
    wout = nc.dram_tensor("wout", [L, 32, 128, 4096], F32, kind="ExternalInput").ap()
    wsr = nc.dram_tensor("wsr", [L, 128, 1280], F32, kind="ExternalInput").ap()
    ppd = nc.dram_tensor("pp", [128, NPP], F32, kind="ExternalInput").ap()
    biasd = nc.dram_tensor("biasb", [L, 128, 1280], F32, kind="ExternalInput").ap()
    miscd = nc.dram_tensor("misc", [128, MISC_N], F32, kind="ExternalInput").ap()
    outT = nc.dram_tensor("outT", [32, 128, ltok - HALO], F32, kind="ExternalOutput").ap()

    es = contextlib.ExitStack()
    with es:
        def sb(name, shape, dt):
            return es.enter_context(nc.sbuf_tensor(name, shape, dt))

        def ps(name, shape, dt=F32):
            return es.enter_context(nc.psum_tensor(name, shape, dt))

        def sem(name):
            return es.enter_context(nc.semaphore(name))

        x = sb("x", [128, 32, TMAX], F32)
        hT = sb("hT", [128, 32, TMAX], BF16)
        ycat = sb("ycat", [128, 32, TMAX], BF16)
        wb = [sb("w%d" % i, [128, 4096], BF16) for i in range(NW)]
        co = sb("co", [128, 12, TMAX], F32)
        hn = sb("hn", [128, 12, TMAX], BF16)
        hbuf = sb("hbuf", [128, HH + TMAX], F32)
        cbufs = [sb("cbuf%d" % i, [128, CH + TMAX], F32) for i in range(2)]
        ptA = sb("ptA", [128, CH + TMAX], F32)
        ptB = sb("ptB", [128, CH + TMAX], F32)
        stg = [sb("stg%d" % i, [128, TMAX], F32) for i in range(7)]
        rstd = sb("rstd", [128, TMAX], F32)
        nb = sb("nb", [128, TMAX], F32)
        nacc = sb("nacc", [128, TMAX], F32)
        mean = nacc
        nrstd = sb("nrstd", [128, TMAX], F32)
        vnb = [sb("vnb%d" % i, [128, TMAX], BF16) for i in range(2)]
        vnT = [sb("vnT%d" % i, [128, TMAX], BF16) for i in range(2)]
        htail = sb("htail", [128, L, 12, HH], F32)
        ctail = sb("ctail", [128, L, 10, CH], F32)
        wsT = sb("wsT", [128, L, 1280], BF16)
        bbuf = [sb("bb%d" % i, [128, 128], F32) for i in range(2)]
        pp = sb("pp_sb", [128, NPP + NX], F32)
        corr_sb = sb("corr_sb", [128, 160], F32)
        identb = sb("identb", [128, 128], BF16)
        onesA = sb("onesA", [128, 128], F32)
        onesB = sb("onesB", [128, 128], F32)
        onesN = sb("onesN", [128, 128], F32)

        print('sbuf bytes remaining', nc.sbuf_bytes_remaining)
        dC = ycat

        proj = [ps("pj%d" % i, [128, 512]) for i in range(3)]
        aux = [ps("ax%d" % i, [128, 512]) for i in range(2)]
        st0 = ps("st0", [128, 512])
        st1 = ps("st1", [128, 512])
        tps = ps("tps", [128, 1024], BF16)

        s_pe, s_act, s_dve, s_pool = sem("s_pe"), sem("s_act"), sem("s_dve"), sem("s_pool")
        s_w = [sem("s_w%d" % i) for i in range(NW)]
        s_p = sem("s_p")
        s_bb = [sem("s_bb%d" % i) for i in range(2)]
        bb_cnt = [0, 0]
        s_pre = [sem("s_pre%d" % i) for i in range(7)]
        pre_cnt = [0] * 7
        s_xg = [sem("s_x%d" % i) for i in range(4)]
        s_og = [sem("s_o%d" % i) for i in range(4)]

        PE = Stream("pe", s_pe)
        ACT = Stream("act", s_act, serial=True)
        DVE = Stream("dve", s_dve, serial=True)
        POOL = Stream("pool", s_pool, serial=True)
        SP = Stream("sp", None)

        PROJ = Ring(proj)
        AUX = Ring(aux)
        STG = Ring(stg)

        def ppc(name, idx):
            o = PP_OFF[name] + idx
            return pp[:, o:o + 1]

        plan = []

        def plan_cout(l, jo):
            plan.append((win[l, S_CGATE + jo], 4096))
            plan.append((wcw[l, jo, :, 0:len(CW_JI[jo]) * 128], len(CW_JI[jo]) * 128))

        for ti in range(len(tiles)):
            for l in range(nlay):
                for j in range(12):
                    if j >= 5:
                        plan_cout(l, j - 5)
                    plan.append((win[l, S_AGLU + j], 4096))
                    plan.append((win[l, S_AVAL + j], 4096))
                    if j < 10:
                        plan.append((win[l, S_CIN + j], 4096))
                for jo in (7, 8, 9):
                    plan_cout(l, jo)
                for j in range(10):
                    plan.append((win[l, S_BV + j], 4096))
                for j in range(12):
                    plan.append((win[l, S_AGATE + j], 4096))
                    plan.append((wpw[l, j], 1536))
                for j in range(10):
                    plan.append((win[l, S_BU + j], 4096))
                    plan.append((win[l, S_BGATE + j], 4096))
                for n in range(32):
                    plan.append((wout[l, n], 4096))

        class WMgr:
            def __init__(self):
                self.issued = 0
                self.consumed = 0
                self.free_tk = [None] * NW
                self.fill = [0] * NW
                self.load_tk = {}

            def issue_upto(self, n):
                while self.issued < min(n, len(plan)):
                    k = self.issued
                    b = k % NW
                    src, ncols = plan[k]
                    self.fill[b] += 1
                    tk = ("w%d" % b, 16 * self.fill[b], s_w[b])
                    dst = wb[b][:, 0:ncols]
                    POOL.emit("dma_start", **dict(out=dst, in_=src, max_dma_last_dim=8192),
                              deps=[self.free_tk[b]], dma=tk)
                    self.load_tk[k] = tk
                    self.issued += 1

            def take(self, ncols):
                k = self.consumed
                self.issue_upto(k + 1)
                assert plan[k][1] == ncols, (k, plan[k][1], ncols)
                return k, wb[k % NW], self.load_tk.pop(k)

            def done(self, k, tk):
                self.free_tk[k % NW] = tk
                self.consumed += 1
                self.issue_upto(k + NW + 1)

        W = WMgr()

        deferred = []
        slab_ctr = [0]

        def defer(delay, fn):
            deferred.append((slab_ctr[0] + delay, fn))

        def flush(force=False):
            while deferred and (force or deferred[0][0] <= slab_ctr[0]):
                _, fn = deferred.pop(0)
                fn()

        def slab_mm(out_ap, ncols, rhs_list, deps, deps_i=None):
            k, buf, ltk = W.take(ncols)
            n = len(rhs_list)
            assert n * 128 == ncols
            tk = None
            for i, rhs in enumerate(rhs_list):
                tk = PE.emit("matmul", **dict(out=out_ap, lhsT=buf[:, i * 128:(i + 1) * 128], rhs=rhs,
                                                                     start=(i == 0), stop=(i == n - 1)),
                             deps=(([ltk] + list(deps)) if i == 0 else []) + ([deps_i[i]] if deps_i else []), inc=(i == n - 1))
            W.done(k, tk)
            slab_ctr[0] += 1
            return tk

        tkp = ("s_p", 16 * 5, s_p)
        SP.emit("dma_start", **dict(out=pp[:, 0:NPP], in_=ppd), dma=("s_p", 16, s_p))
        SP.emit("dma_start", **dict(out=stg[0][:, 0:256], in_=miscd[:, 0:256]), dma=("s_p", 32, s_p))
        SP.emit("dma_start", **dict(out=corr_sb[:], in_=miscd[:, 256:416]), dma=("s_p", 48, s_p))
        costage = co[:, 0:8, 0:320]
        SP.emit("dma_start", **dict(out=co[:, 0:4, 0:320], in_=wsr[0].rearrange("q (a b) -> q a b", a=4)), dma=("s_p", 64, s_p))
        SP.emit("dma_start", **dict(out=co[:, 4:8, 0:320], in_=wsr[1].rearrange("q (a b) -> q a b", a=4)), dma=tkp)
        W.issue_upto(NW)
        tk0 = DVE.emit("tensor_copy", **dict(out=identb[:], in_=stg[0][:, 0:128]), deps=[tkp])
        DVE.emit("memset", **dict(ap=onesA[:], constant=1.0 / WA))
        DVE.emit("memset", **dict(ap=onesB[:], constant=1.0 / WB))
        DVE.emit("memset", **dict(ap=onesN[:], constant=1.0 / D))
        DVE.emit("memset", **dict(ap=htail[:], constant=0.0))
        tk_init = DVE.emit("memset", **dict(ap=ctail[:], constant=0.0))
        tk_half = DVE.emit("tensor_scalar", out=pp[:, PP_OFF["cw"]:PP_OFF["cw"] + L * 12 * CONVW],
                           in0=pp[:, PP_OFF["cw"]:PP_OFF["cw"] + L * 12 * CONVW], scalar1=0.5, scalar2=None, op0=ALU.mult, deps=[tkp])
        tk_ws = None
        for l in range(L):
            for h in range(10):
                a, r = divmod(h * 128, 320)
                for p0 in range(0, 128, 64):
                    col = h * 128 + p0
                    a, r = divmod(col, 320)
                    tk_ws = DVE.emit("tensor_tensor", **dict(
                        out=wsT[:, l, col:col + 64], in0=co[:, 4 * l + a, r:r + 64], in1=stg[0][:, 128 + p0:128 + p0 + 64], op=ALU.mult),
                        deps=[tkp])
        STG.release(0, tk_ws)
        co_free = [tk_ws]
        hT_free = [None]
        ycat_free = [None]
        stat_free = [[], []]
        htail_tk = [[tk_init] * 12 for _ in range(L)]
        ctail_tk = [[tk_init] * 10 for _ in range(L)]
        x_state = {"loads": 0, "stores": 0, "store_tk": [None] * 4}

        def norm_accum(n, T, xtk, st):
            k, sbuf_, sdeps = STG.get()
            tsq = ACT.emit("activation", **dict(out=sbuf_[:, :T], in_=x[:, n, :T], func=AF.Square), deps=[xtk] + sdeps)
            if n == 0:
                t = DVE.emit("tensor_copy", **dict(out=nacc[:, :T], in_=sbuf_[:, :T]), deps=[tsq] + st["nacc_free"])
            else:
                t = DVE.emit("tensor_tensor", **dict(out=nacc[:, :T], in0=nacc[:, :T], in1=sbuf_[:, :T], op=ALU.add),
                             deps=[tsq, st["nacc_tk"]])
            st["nacc_tk"] = t
            STG.release(k, t)

        def norm_finalize(T, st, acc=None, acc_tk=None):
            if acc is None:
                acc, acc_tk = nacc, st["nacc_tk"]
            tpe = PE.emit("matmul", **dict(out=st0[:, :T], lhsT=onesN[:], rhs=acc[:, :T], start=True, stop=True),
                          deps=[acc_tk] + stat_free[0])
            stat_free[0] = []
            if acc is nacc:
                st["nacc_free"] = [tpe]
            ta = ACT.emit("activation", out=nrstd[:, :T], in_=st0[:, :T], func=AF.Sqrt, bias=EPS, scale=1.0,
                          deps=[tpe, st.get("nrstd_free")])
            stat_free[0].append(ta)
            td = DVE.emit("reciprocal", out=nrstd[:, :T], in_=nrstd[:, :T], deps=[ta])
            return td

        nst = {"nacc_free": [], "nacc_tk": None}
        pers = {}

        for ti, (t0, T) in enumerate(tiles):
            C = T // 128
            x_state["loads"] += 1
            xg_tk = []
            for q in range(4):
                tkq = ("s_x%d" % q, 16 * x_state["loads"], s_xg[q])
                SP.emit("dma_start", **dict(out=x[:, 8 * q:8 * q + 8, 0:T],
                                            in_=xT[8 * q:8 * q + 8, :, t0:t0 + T].rearrange("c p t -> p c t")),
                        deps=[x_state["store_tk"][q]], dma=tkq)
                xg_tk.append(tkq)
            if pers.get("pre_tk") is not None:
                rs_tk = norm_finalize(T, nst, acc=rstd, acc_tk=pers["pre_tk"])
                pers["pre_tk"] = None
            else:
                for n in range(32):
                    norm_accum(n, T, xg_tk[n // 8], nst)
                rs_tk = norm_finalize(T, nst)
            x_tk = [xg_tk[n // 8] for n in range(32)]

            for l in range(nlay):
                btks = []
                for j in range(10):
                    bb_cnt[j % 2] += 1
                    btk_j = ("s_bb%d" % (j % 2), 16 * bb_cnt[j % 2], s_bb[j % 2])
                    btks.append(btk_j)

                ht_tk = None
                ht_tks = []
                for n in range(32):
                    ht_tk = DVE.emit("scalar_tensor_tensor", **dict(
                        out=hT[:, n, :T], in0=x[:, n, :T], scalar=ppc("ng", l * 32 + n), in1=nrstd[:, :T],
                        op0=ALU.mult, op1=ALU.mult), deps=[rs_tk, x_tk[n], hT_free[0], tkp])
                    ht_tks.append(ht_tk)
                nst["nrstd_free"] = ht_tk
                hT_rhs = [hT[:, kc, :T] for kc in range(32)]

                conv_tk = [None] * 12
                dC_tk = [None] * 10
                lnA = {}

                def ln_stats(j, nch, ones, src_tk, holder, T=T):
                    k, sbuf_, sdeps = STG.get()
                    tsq = ACT.emit("activation", out=sbuf_[:, :T], in_=co[:, j, :T], func=AF.Square, deps=[src_tk] + sdeps)
                    if j == 0:
                        a1 = POOL.emit("tensor_copy", out=nacc[:, :T], in_=co[:, j, :T],
                                       deps=[src_tk, pers.get("mean_free")] + nst["nacc_free"])
                        a2 = POOL.emit("tensor_copy", out=nb[:, :T], in_=sbuf_[:, :T], deps=[tsq, pers.get("nb_free")])
                    else:
                        a1 = POOL.emit("tensor_tensor", out=nacc[:, :T], in0=nacc[:, :T], in1=co[:, j, :T], op=ALU.add, deps=[src_tk])
                        a2 = POOL.emit("tensor_tensor", out=nb[:, :T], in0=nb[:, :T], in1=sbuf_[:, :T], op=ALU.add, deps=[tsq])
                    STG.release(k, a2)
                    if j == nch - 1:
                        d0, d1 = stat_free[0], stat_free[1]
                        stat_free[0] = []
                        stat_free[1] = []
                        PE.emit("matmul", out=st0[:, :T], lhsT=ones[:], rhs=nacc[:, :T], start=True, stop=True, deps=[a1] + d0, inc=False)
                        te = PE.emit("matmul", out=st1[:, :T], lhsT=ones[:], rhs=nb[:, :T], start=True, stop=True, deps=[a2] + d1)
                        holder["pe"] = te
                        nst["nacc_free"] = [te]

                def lnA_stats(j):
                    ln_stats(j, 12, onesA, conv_tk[j], lnA)

                def ln_finalize(pe_tk, T):
                    t1 = DVE.emit("tensor_copy", **dict(out=mean[:, :T], in_=st0[:, :T]), deps=[pe_tk] + nst["nacc_free"])
                    t2 = DVE.emit("tensor_tensor", out=rstd[:, :T], in0=mean[:, :T], in1=mean[:, :T], op=ALU.mult, deps=[t1])
                    t3 = DVE.emit("tensor_tensor", out=rstd[:, :T], in0=st1[:, :T], in1=rstd[:, :T], op=ALU.subtract, deps=[t2])
                    t3b = DVE.emit("tensor_scalar", out=rstd[:, :T], in0=rstd[:, :T], scalar1=0.0, scalar2=None, op0=ALU.max, deps=[t3])
                    ta = ACT.emit("activation", out=rstd[:, :T], in_=rstd[:, :T], func=AF.Sqrt, bias=EPS, scale=1.0, deps=[t3b])
                    t4 = DVE.emit("reciprocal", **dict(out=rstd[:, :T], in_=rstd[:, :T]), deps=[ta])
                    t5 = DVE.emit("scalar_tensor_tensor", **dict(out=nb[:, :T], in0=mean[:, :T], scalar=-1.0, in1=rstd[:, :T],
                                                                   op0=ALU.mult, op1=ALU.mult), deps=[t4])
                    stat_free[0].append(t1)
                    stat_free[1].append(t3)
                    pers["mean_free"] = t5
                    return t5

                def lnA_finish(T=T, l=l):
                    t5 = ln_finalize(lnA["pe"], T)
                    last = None
                    for j in range(12):
                        a = DVE.emit("tensor_tensor", **dict(out=co[:, j, :T], in0=co[:, j, :T], in1=rstd[:, :T], op=ALU.mult),
                                     deps=[t5, lnA["pe"]])
                        b = DVE.emit("tensor_tensor", **dict(out=co[:, j, :T], in0=co[:, j, :T], in1=nb[:, :T], op=ALU.add),
                                     deps=[a])
                        pers["nb_free"] = b
                        last = lnA.setdefault("hn_tks", [None] * 12)[j] = ACT.emit("activation", **dict(out=hn[:, j, :T], in_=co[:, j, :T], func=AF.Silu,
                                                                    bias=ppc("alb", l * 12 + j), scale=ppc("alg", l * 12 + j)),
                                        deps=[b, pers.get("hn_free")])
                    lnA["hn_tk"] = last

                cw_state = {"last": None}
                yc_tk = [None] * 32

                def c_out(jo, T=T, l=l):
                    pk, pbuf, pdeps = PROJ.get()
                    tpe = slab_mm(pbuf[:, :T], 4096, hT_rhs, pdeps)
                    sk, sbuf_, sdeps = STG.get()
                    tsg = ACT.emit("activation", out=sbuf_[:, :T], in_=pbuf[:, :T], func=AF.Silu, deps=[tpe] + sdeps)
                    PROJ.release(pk, tsg)
                    ak, abuf, adeps = AUX.get()
                    jis = CW_JI[jo]
                    tcw = slab_mm(abuf[:, :T], len(jis) * 128, [dC[:, 12 + ji, :T] for ji in jis],
                                  [dC_tk[ji] for ji in jis] + adeps)
                    cw_state["last"] = tcw
                    ty = DVE.emit("scalar_tensor_tensor", out=ycat[:, 22 + jo, :T], in0=abuf[:, :T], scalar=ppc("csc", l * 10 + jo),
                                  in1=sbuf_[:, :T], op0=ALU.mult, op1=ALU.mult, deps=[tcw, tsg, ycat_free[0], tkp])
                    yc_tk[22 + jo] = ty
                    AUX.release(ak, ty)
                    STG.release(sk, ty)
                    flush()

                for j in range(12):
                    if j >= 5:
                        c_out(j - 5)
                    pk, pbuf, pdeps = PROJ.get()
                    tpe = slab_mm(pbuf[:, :T], 4096, hT_rhs, pdeps, deps_i=(ht_tks if j == 0 else None))
                    sk, sbuf_, sdeps = STG.get()
                    tsig = ACT.emit("activation", out=sbuf_[:, :T], in_=pbuf[:, :T], func=AF.Tanh, scale=0.5, deps=[tpe] + sdeps)
                    PROJ.release(pk, tsig)
                    flush()
                    pk2, pbuf2, pdeps2 = PROJ.get()
                    tpe2 = slab_mm(pbuf2[:, :T], 4096, hT_rhs, pdeps2)
                    skv, vbuf_, sdepsv = STG.get()
                    tval = ACT.emit("activation", out=vbuf_[:, :T], in_=pbuf2[:, :T], func=AF.Copy, deps=[tpe2] + sdepsv)
                    PROJ.release(pk2, tval)
                    th0 = DVE.emit("tensor_copy", out=hbuf[:, 0:HH], in_=htail[:, l, j, :], deps=[htail_tk[l][j]])
                    tglu = DVE.emit("scalar_tensor_tensor", out=hbuf[:, HH:HH + T], in0=sbuf_[:, :T], scalar=1.0, in1=vbuf_[:, :T],
                                    op0=ALU.add, op1=ALU.mult, deps=[tval, tsig])
                    STG.release(sk, tglu)
                    STG.release(skv, tglu)
                    htail_tk[l][j] = DVE.emit("tensor_copy", out=htail[:, l, j, :], in_=hbuf[:, T:T + HH], deps=[tglu])
                    base = (l * 12 + j) * CONVW
                    skc, accB, sdepsc = STG.get()
                    tA = DVE.emit("tensor_scalar", out=co[:, j, :T], in0=hbuf[:, 2:2 + T], scalar1=ppc("cwh", base),
                                  scalar2=ppc("cb", l * 12 + j), op0=ALU.mult, op1=ALU.add, deps=[th0, tglu, tk_half] + co_free)
                    co_free = []
                    tB = DVE.emit("tensor_scalar", out=accB[:, :T], in0=hbuf[:, 18:18 + T], scalar1=ppc("cwh", base + 16),
                                  scalar2=None, op0=ALU.mult, deps=[th0, tglu] + sdepsc)
                    for i in range(1, 16):
                        tA = DVE.emit("scalar_tensor_tensor", out=co[:, j, :T], in0=hbuf[:, 2 + i:2 + i + T], scalar=ppc("cwh", base + i),
                                      in1=co[:, j, :T], op0=ALU.mult, op1=ALU.add, deps=[tA], noserial=True)
                        if 16 + i < CONVW:
                            tB = DVE.emit("scalar_tensor_tensor", out=accB[:, :T], in0=hbuf[:, 18 + i:18 + i + T],
                                          scalar=ppc("cwh", base + 16 + i), in1=accB[:, :T], op0=ALU.mult, op1=ALU.add,
                                          deps=[tB], noserial=True)
                    tk = DVE.emit("tensor_tensor", out=co[:, j, :T], in0=co[:, j, :T], in1=accB[:, :T], op=ALU.add, deps=[tA, tB])
                    STG.release(skc, tk)
                    conv_tk[j] = tk
                    defer(9, lambda j=j: lnA_stats(j))
                    flush()
                    if j < 10:
                        pk3, pbuf3, pdeps3 = PROJ.get()
                        tpe3 = slab_mm(pbuf3[:, :T], 4096, hT_rhs, pdeps3)
                        cbuf = cbufs[j % 2]
                        tc1 = ACT.emit("activation", out=cbuf[:, CH:CH + T], in_=pbuf3[:, :T], func=AF.Copy,
                                       deps=[tpe3, pers.get("cbuf_free%d" % (j % 2))])
                        PROJ.release(pk3, tc1)
                        tc0 = POOL.emit("tensor_copy", out=cbuf[:, 0:CH], in_=ctail[:, l, j, :],
                                        deps=[ctail_tk[l][j], pers.get("cbuf_free%d" % (j % 2))])
                        ctail_tk[l][j] = POOL.emit("tensor_copy", out=ctail[:, l, j, :], in_=cbuf[:, T:T + CH], deps=[tc1])
                        gs = _chunk_groups(j)
                        parts = [(0, 128, gs[0])] if len(gs) == 1 else [(0, 64, gs[0]), (64, 128, gs[1])]
                        E = CH + T
                        srcs = [cbuf, ptA, ptB, ptA, ptB]
                        prev = [tc0, tc1]
                        final_src = {}
                        for step, sh in enumerate((1, 2, 4, 8)):
                            wnd = 2 * sh
                            act_parts = [(a_, b_) for (a_, b_, g) in parts if POOLW[g] >= wnd]
                            if not act_parts:
                                break
                            lo, hi = min(a_ for a_, b_ in act_parts), max(b_ for a_, b_ in act_parts)
                            s_in, s_out = srcs[step], srcs[step + 1]
                            st_ = wnd - 1
                            tkp_ = POOL.emit("tensor_tensor", out=s_out[lo:hi, st_:E], in0=s_in[lo:hi, st_:E],
                                             in1=s_in[lo:hi, st_ - sh:E - sh], op=ALU.add, deps=prev + [pers.get("pt_free")])
                            prev = [tkp_]
                            for (a_, b_, g) in parts:
                                if POOLW[g] == wnd:
                                    final_src[(a_, b_)] = s_out
                        tlast = None
                        for (a_, b_, g) in parts:
                            fs = final_src[(a_, b_)]
                            if ti == 0:
                                c0 = CH + HALO
                                tfix = DVE.emit("tensor_tensor", out=fs[a_:b_, c0:c0 + 16], in0=fs[a_:b_, c0:c0 + 16],
                                                in1=corr_sb[a_:b_, j * 16:j * 16 + 16], op=ALU.mult, deps=prev + [tkp])
                                prev = [tfix]
                            tlast = DVE.emit("scalar_tensor_tensor", out=dC[a_:b_, 12 + j, :T], in0=fs[a_:b_, CH:CH + T],
                                             scalar=pp[a_:b_, PP_OFF["invw"] + j:PP_OFF["invw"] + j + 1],
                                             in1=cbuf[a_:b_, CH:CH + T], op0=ALU.mult, op1=ALU.subtract, deps=prev + [ycat_free[0]])
                            prev = [tlast]
                        dC_tk[j] = tlast
                        pers["cbuf_free%d" % (j % 2)] = tlast
                        pers["pt_free"] = tlast
                        flush()
                defer(10, lnA_finish)
                for jo in (7, 8, 9):
                    c_out(jo)
                flush(force=True)
                cw_last = cw_state["last"]

                v_tk = [None] * 10
                lnB = {}

                def lnB_stats(j):
                    ln_stats(j, 10, onesB, v_tk[j], lnB)

                for j in range(10):
                    pk, pbuf, pdeps = PROJ.get()
                    tpe = slab_mm(pbuf[:, :T], 4096, hT_rhs, pdeps)
                    v_tk[j] = ACT.emit("activation", out=co[:, j, :T], in_=pbuf[:, :T], func=AF.Gelu_apprx_tanh,
                                       deps=[tpe, lnA["hn_tks"][j]])
                    PROJ.release(pk, v_tk[j])
                    defer(1, lambda j=j: lnB_stats(j))
                    flush()

                def vn_make(j, t5, T=T, l=l):
                    a = DVE.emit("tensor_tensor", out=co[:, j, :T], in0=co[:, j, :T], in1=rstd[:, :T], op=ALU.mult,
                                 deps=[t5, lnB["pe"]])
                    b = DVE.emit("tensor_tensor", out=co[:, j, :T], in0=co[:, j, :T], in1=nb[:, :T], op=ALU.add, deps=[a])
                    pers["nb_free"] = b
                    vb = vnb[j % 2]
                    c = ACT.emit("activation", out=vb[:, :T], in_=co[:, j, :T], func=AF.Identity,
                                 bias=ppc("blb", l * 10 + j), scale=ppc("blg", l * 10 + j),
                                 deps=[b, pers.get("vnb_free%d" % (j % 2))])
                    return c

                pw_last = None
                t5B = None
                vn_tk = [None] * 10
                for j in range(12):
                    pk, pbuf, pdeps = PROJ.get()
                    tpe = slab_mm(pbuf[:, :T], 4096, hT_rhs, pdeps)
                    flush()
                    if j == 0:
                        flush(force=True)
                        t5B = ln_finalize(lnB["pe"], T)
                        vn_tk[0] = vn_make(0, t5B)
                        vn_tk[1] = vn_make(1, t5B)
                    sk, sbuf_, sdeps = STG.get()
                    tsg = ACT.emit("activation", out=sbuf_[:, :T], in_=pbuf[:, :T], func=AF.Silu, deps=[tpe] + sdeps)
                    PROJ.release(pk, tsg)
                    ak, abuf, adeps = AUX.get()
                    tpw = slab_mm(abuf[:, :T], 1536, [hn[:, kc, :T] for kc in range(12)], [lnA["hn_tk"]] + adeps)
                    pw_last = tpw
                    ty = DVE.emit("tensor_tensor", out=ycat[:, j, :T], in0=abuf[:, :T], in1=sbuf_[:, :T], op=ALU.mult,
                                  deps=[tpw, tsg, ycat_free[0]])
                    yc_tk[j] = ty
                    AUX.release(ak, ty)
                    STG.release(sk, ty)
                    flush()
                pers["hn_free"] = pw_last

                hb = {}
                sp_state = {"last": None}

                def spatial_and_out(j, T=T, l=l, C=C):
                    sk, gu, tm, vt, tev = hb.pop(j)
                    ak, abuf, adeps = AUX.get()
                    tsp = None
                    for c in range(C):
                        tsp = PE.emit("matmul", out=abuf[:, c * 128:(c + 1) * 128], lhsT=vt[:, c * 128:(c + 1) * 128],
                                      rhs=wsT[:, l, j * 128:(j + 1) * 128], start=True, stop=True,
                                      deps=([tev, tk_ws] + adeps) if c == 0 else (), inc=(c == C - 1))
                    pers["vnT_free%d" % (j % 2)] = tsp
                    sp_state["last"] = tsp
                    sk3, tb, sdeps3 = STG.get()
                    bb = bbuf[j % 2]
                    SP.emit("dma_start", out=bb[:], in_=biasd[l, :, j * 128:(j + 1) * 128],
                            deps=[pers.get("bb_free%d" % (j % 2))], dma=btks[j])
                    tadd = None
                    for c in range(C):
                        tadd = DVE.emit("tensor_tensor", out=tb[:, c * 128:(c + 1) * 128], in0=abuf[:, c * 128:(c + 1) * 128],
                                        in1=bb[:], op=ALU.add,
                                        deps=([tsp, btks[j]] + sdeps3) if c == 0 else ())
                    pers["bb_free%d" % (j % 2)] = tadd
                    AUX.release(ak, tadd)
                    ty = DVE.emit("scalar_tensor_tensor", out=ycat[:, 12 + j, :T], in0=tb[:, :T], scalar=0.5, in1=gu[:, :T],
                                  op0=ALU.mult, op1=ALU.mult, deps=[tadd, tm, cw_last])
                    yc_tk[12 + j] = ty
                    STG.release(sk, ty)
                    STG.release(sk3, ty)

                for j in range(10):
                    pk, pbuf, pdeps = PROJ.get()
                    tpe = slab_mm(pbuf[:, :T], 4096, hT_rhs, pdeps)
                    sk, gu, sdeps = STG.get()
                    tgu = ACT.emit("activation", out=gu[:, :T], in_=pbuf[:, :T], func=AF.Gelu_apprx_tanh, deps=[tpe] + sdeps)
                    PROJ.release(pk, tgu)
                    if j >= 1:
                        spatial_and_out(j - 1)
                    pk2, pbuf2, pdeps2 = PROJ.get()
                    tpe2 = slab_mm(pbuf2[:, :T], 4096, hT_rhs, pdeps2)
                    sk2, th, sdeps2 = STG.get()
                    tth = ACT.emit("activation", out=th[:, :T], in_=pbuf2[:, :T], func=AF.Tanh, scale=0.5, deps=[tpe2] + sdeps2)
                    tsg = DVE.emit("scalar_tensor_tensor", out=th[:, :T], in0=th[:, :T], scalar=1.0, in1=pbuf2[:, :T],
                                   op0=ALU.add, op1=ALU.mult, deps=[tth])
                    PROJ.release(pk2, tsg)
                    tm = DVE.emit("tensor_tensor", out=gu[:, :T], in0=gu[:, :T], in1=th[:, :T], op=ALU.mult, deps=[tsg, tgu])
                    STG.release(sk2, tm)
                    vb = vnb[j % 2]
                    ttp = None
                    for c in range(C):
                        ttp = PE.emit("transpose", out=tps[:, c * 128:(c + 1) * 128], in_=vb[:, c * 128:(c + 1) * 128], identity=identb[:],
                                      deps=[vn_tk[j], tk0, pers.get("tps_free")] if c == 0 else (), inc=(c == C - 1))
                    pers["vnb_free%d" % (j % 2)] = ttp
                    vt = vnT[j % 2]
                    tev = DVE.emit("tensor_copy", out=vt[:, :T], in_=tps[:, :T], deps=[ttp, pers.get("vnT_free%d" % (j % 2))])
                    pers["tps_free"] = tev
                    hb[j] = (sk, gu, tm, vt, tev)
                    if j + 2 < 10:
                        vn_tk[j + 2] = vn_make(j + 2, t5B)
                    flush()
                spatial_and_out(9)
                hT_free[0] = ("pe", PE.count, s_pe)
                co_free = [sp_state["last"], lnB["pe"]]

                ycat_rhs = [ycat[:, kc, :T] for kc in range(32)]
                wo_last = None
                pend = None
                for n in range(32):
                    pk, pbuf, pdeps = PROJ.get()
                    tpe = slab_mm(pbuf[:, :T], 4096, ycat_rhs, (yc_tk if n == 0 else []) + pdeps)
                    wo_last = tpe
                    tx = DVE.emit("tensor_tensor", **dict(out=x[:, n, :T], in0=pbuf[:, :T], in1=x[:, n, :T], op=ALU.add),
                                  deps=[tpe, ht_tk])
                    PROJ.release(pk, tx)
                    x_tk[n] = tx
                    norm_accum(n, T, tx, nst)
                    if l == nlay - 1 and ti + 1 < len(tiles):
                        nt0, nT = tiles[ti + 1]
                        k, sbuf_, sdeps = STG.get()
                        pre_cnt[k] += 1
                        tld = ("s_pre%d" % k, 16 * pre_cnt[k], s_pre[k])
                        SP.emit("dma_start", out=sbuf_[:, :nT], in_=xT[n, :, nt0:nt0 + nT], deps=sdeps, dma=tld)
                        tsq = ACT.emit("activation", out=sbuf_[:, :nT], in_=sbuf_[:, :nT], func=AF.Square, deps=[tld])
                        if n == 0:
                            tacc = DVE.emit("tensor_copy", out=rstd[:, :nT], in_=sbuf_[:, :nT], deps=[tsq])
                        else:
                            tacc = DVE.emit("tensor_tensor", out=rstd[:, :nT], in0=rstd[:, :nT], in1=sbuf_[:, :nT], op=ALU.add, deps=[tsq])
                        STG.release(k, tacc)
                        pers["pre_tk"] = tacc
                    flush()
                ycat_free[0] = wo_last
                rs_tk = norm_finalize(T, nst)

            skip = HALO if ti == 0 else 0
            o0 = t0 + skip - HALO
            To = T - skip
            x_state["stores"] += 1
            for q in range(4):
                fin = None
                for n in range(8 * q, 8 * q + 8):
                    fin = DVE.emit("scalar_tensor_tensor", **dict(
                        out=x[:, n, :T], in0=x[:, n, :T], scalar=ppc("fg", n), in1=nrstd[:, :T], op0=ALU.mult, op1=ALU.mult),
                        deps=[rs_tk, x_tk[n]])
                nst["nrstd_free"] = fin
                tkq = ("s_o%d" % q, 16 * x_state["stores"], s_og[q])
                SP.emit("dma_start", **dict(
                    out=outT[8 * q:8 * q + 8, :, o0:o0 + To].rearrange("c p t -> p c t"), in_=x[:, 8 * q:8 * q + 8, skip:skip + To]),
                    deps=[fin], dma=tkq)
                x_state["store_tk"][q] = tkq

        assert W.consumed == len(plan), (W.consumed, len(plan))
        final_waits = list(x_state["store_tk"])

        with nc.Block() as block:
            @block.sync
            def _(e):
                SP.replay(e)
                for fw in final_waits:
                    e.wait_ge(fw[2], fw[1])

            @block.gpsimd
            def _(e):
                POOL.replay(e)

            @block.tensor
            def _(e):
                PE.replay(e)

            @block.scalar
            def _(e):
                ACT.replay(e)

            @block.vector
            def _(e):
                DVE.replay(e)
    return nc


def _prep_shared(norm_g, w_in, conv_w, conv_b, a_ln_g, a_ln_b, a_pw, b_ln_g, b_ln_b, b_ws, b_bias, c_w, c_scale, w_out, final_g):
    f = np.float32
    win = np.ascontiguousarray(np.asarray(w_in, f).reshape(L, 32, 128, 86, 128).transpose(0, 3, 2, 1, 4)).reshape(L, 86, 128, 4096)
    wpw = np.ascontiguousarray(np.asarray(a_pw, f).reshape(L, 12, 128, 12, 128).transpose(0, 3, 2, 1, 4)).reshape(L, 12, 128, 1536)
    wout = np.ascontiguousarray(np.asarray(w_out, f).reshape(L, 32, 128, 32, 128).transpose(0, 3, 2, 1, 4)).reshape(L, 32, 128, 4096)
    cwn = np.asarray(c_w, f)
    wbd = np.zeros((L, WC, WC), f)
    for g in range(4):
        wbd[:, g * 320:(g + 1) * 320, g * 320:(g + 1) * 320] = cwn[:, g]
    wcw = np.zeros((L, 10, 128, 640), f)
    for jo in range(10):
        for i, ji in enumerate(CW_JI[jo]):
            wcw[:, jo, :, i * 128:(i + 1) * 128] = wbd[:, ji * 128:(ji + 1) * 128, jo * 128:(jo + 1) * 128]
    wsr = np.ascontiguousarray(np.asarray(b_ws, f).transpose(0, 3, 1, 2)).reshape(L, 128, 1280)
    biasb = np.ascontiguousarray(np.broadcast_to(np.asarray(b_bias, f).reshape(L, 1, 1280), (L, 128, 1280)))
    pp = np.zeros((128, NPP), f)

    def put(name, arr):
        o = PP_OFF[name]
        pp[:, o:o + arr.shape[1]] = arr
    put("cw", np.asarray(conv_w, f).reshape(L, CONVW, 12, 128).transpose(3, 0, 2, 1).reshape(128, -1))
    for name, arr, nch in (("cb", conv_b, 12), ("alg", a_ln_g, 12), ("alb", a_ln_b, 12), ("blg", b_ln_g, 10),
                           ("blb", b_ln_b, 10), ("csc", c_scale, 10), ("ng", norm_g, 32)):
        put(name, np.asarray(arr, f).reshape(L, nch, 128).transpose(2, 0, 1).reshape(128, -1))
    put("fg", np.asarray(final_g, f).reshape(32, 128).T)
    ch = np.arange(WC)
    wnd = np.array(POOLW, f)[ch // 320]
    put("invw", (1.0 / wnd).astype(f).reshape(10, 128).T)
    return dict(win=win, wpw=wpw, wcw=wcw, wout=wout, wsr=wsr, biasb=biasb, pp=pp)


def _misc(seq_start):
    f = np.float32
    m = np.zeros((128, MISC_N), f)
    m[:, 0:128] = np.eye(128, dtype=f)
    q = np.arange(128)
    m[:, 128:256] = (q[:, None] <= q[None, :]).astype(f)
    ch = np.arange(WC)
    wnd = np.array(POOLW, f)[ch // 320]
    t = np.arange(16, dtype=f)
    if seq_start:
        corr = wnd[:, None] / np.minimum(t[None, :] + 1.0, wnd[:, None])
    else:
        corr = np.ones((WC, 16), f)
    m[:, 256:416] = corr.astype(f).reshape(10, 128, 16).transpose(1, 0, 2).reshape(128, 160)
    return m


_NC_CACHE = {}


def kernel(x, norm_g, w_in, conv_w, conv_b, a_ln_g, a_ln_b, a_pw, b_ln_g, b_ln_b, b_ws, b_bias, c_w, c_scale, w_out, final_g):
    x = np.asarray(x, np.float32)
    B, S, _ = x.shape
    shared = _prep_shared(norm_g, w_in, conv_w, conv_b, a_ln_g, a_ln_b, a_pw, b_ln_g, b_ln_b, b_ws, b_bias, c_w, c_scale, w_out, final_g)
    in_maps = []
    for c in range(NCORE):
        b, qd = divmod(c, NCORE // B)
        s0 = qd * TOK
        xt = np.zeros((LTOK, D), np.float32)
        if s0 > 0:
            xt[:] = x[b, s0 - HALO:s0 + TOK]
        else:
            xt[HALO:] = x[b, 0:TOK]
        m = dict(shared)
        m["xT"] = np.ascontiguousarray(xt.T).reshape(32, 128, LTOK)
        m["misc"] = _misc(s0 == 0)
        in_maps.append(m)
    if "nc" not in _NC_CACHE:
        _NC_CACHE["nc"] = build_nc()
    nc = _NC_CACHE["nc"]
    res = run_bass_kernel_spmd(nc, in_maps, core_ids=list(range(NCORE)))
    out = np.empty((B, S, D), np.float32)
    for c in range(NCORE):
        b, qd = divmod(c, NCORE // B)
        s0 = qd * TOK
        o = np.asarray(res.results[c]["outT"]).reshape(D, TOK)
        out[b, s0:s0 + TOK] = o.T
    return out
```
